# Optimizing a Trainium2 kernel written in Bass

```python
import jax, jax.numpy as jnp
from jax import lax
import numpy as np

D_MODEL = 1024
BATCH = 2
SEQ = 8192
DEPTH = 4

MEM_LEN = 256
HEAD_DIM = 64
N_CONV_GROUPS = 6
CONV_CH = N_CONV_GROUPS * HEAD_DIM
CONV_WIDTH = 31
N_Q_HEADS = 6
N_KV_HEADS = 2
SWA_Q = N_Q_HEADS * HEAD_DIM
SWA_KV = N_KV_HEADS * HEAD_DIM
WINDOW = 128
BLOCK = 128
N_MEM_HEADS = 4
MEM_W = N_MEM_HEADS * HEAD_DIM
D_MIX = CONV_CH + SWA_Q + MEM_W
D_IN = 2 * CONV_CH + SWA_Q + 2 * SWA_KV + MEM_W
D_FF = 2816
ROPE_THETA = 10000.0
EPS = 1e-6

kernel_name = "hymba_style_conformer_swa_memory_macaron"


def rms_norm(x, g):
    xf = x.astype(jnp.float32)
    y = xf * lax.rsqrt(jnp.mean(xf * xf, axis=-1, keepdims=True) + EPS)
    return (y * g.astype(jnp.float32)).astype(x.dtype)


def layer_norm(x, g, b):
    xf = x.astype(jnp.float32)
    mu = jnp.mean(xf, axis=-1, keepdims=True)
    var = jnp.mean(jnp.square(xf - mu), axis=-1, keepdims=True)
    y = (xf - mu) * lax.rsqrt(var + EPS)
    return (y * g.astype(jnp.float32) + b.astype(jnp.float32)).astype(x.dtype)


def swiglu(x, w1, w3, w2):
    return (jax.nn.silu(x @ w1) * (x @ w3)) @ w2


def rope_tables(positions):
    inv_freq = ROPE_THETA ** (-jnp.arange(0, HEAD_DIM, 2, dtype=jnp.float32) / HEAD_DIM)
    ang = positions.astype(jnp.float32)[..., None] * inv_freq
    return jnp.cos(ang)[:, :, None, :], jnp.sin(ang)[:, :, None, :]


def apply_rope(x, cos, sin):
    xf = x.astype(jnp.float32)
    x1, x2 = jnp.split(xf, 2, axis=-1)
    out = jnp.concatenate([x1 * cos - x2 * sin, x2 * cos + x1 * sin], axis=-1)
    return out.astype(x.dtype)


def conv_module(u, w_dw, b_dw, ln_g, ln_b):
    a, gate = jnp.split(u, 2, axis=-1)
    y = a * jax.nn.sigmoid(gate)
    y = lax.conv_general_dilated(
        y, w_dw[:, None, :], window_strides=(1,),
        padding=[(CONV_WIDTH - 1, 0)],
        dimension_numbers=('NWC', 'WIO', 'NWC'),
        feature_group_count=CONV_CH) + b_dw
    y = layer_norm(y, ln_g, ln_b)
    return jax.nn.silu(y)


def sliding_window_attention(q, k, v, sinks):
    B, S, _, Dh = q.shape
    nb = S // BLOCK
    g = N_Q_HEADS // N_KV_HEADS
    qb = q.reshape(B, nb, BLOCK, N_KV_HEADS, g, Dh)
    kb = k.reshape(B, nb, BLOCK, N_KV_HEADS, Dh)
    vb = v.reshape(B, nb, BLOCK, N_KV_HEADS, Dh)

    def with_prev(t):
        prev = jnp.concatenate([jnp.zeros_like(t[:, :1]), t[:, :-1]], axis=1)
        return jnp.concatenate([prev, t], axis=2)

    kw, vw = with_prev(kb), with_prev(vb)
    scores = jnp.einsum('bnqhgd,bnkhd->bnhgqk', qb, kw).astype(jnp.float32) * (Dh ** -0.5)
    qi = jnp.arange(BLOCK)[:, None] + BLOCK
    kj = jnp.arange(2 * BLOCK)[None, :]
    rel = qi - kj
    band = (rel >= 0) & (rel < WINDOW)
    first_ok = (jnp.arange(nb)[:, None, None] > 0) | (kj[None] >= BLOCK)
    mask = band[None] & first_ok
    scores = jnp.where(mask[None, :, None, None], scores, -jnp.inf)
    sink = jnp.broadcast_to(
        sinks.astype(jnp.float32).reshape(N_KV_HEADS, g)[None, None, :, :, None, None],
        scores.shape[:-1] + (1,))
    probs = jax.nn.softmax(jnp.concatenate([scores, sink], axis=-1), axis=-1)[..., :-1]
    out = jnp.einsum('bnhgqk,bnkhd->bnqhgd', probs.astype(v.dtype), vw)
    return out.reshape(B, S, N_Q_HEADS * Dh)


def memory_attention(q, mk, mv):
    B, S, _, Dh = q.shape
    scores = jnp.einsum('bshd,bmhd->bhsm', q, mk).astype(jnp.float32) * (Dh ** -0.5)
    probs = jax.nn.softmax(scores, axis=-1)
    out = jnp.einsum('bhsm,bmhd->bshd', probs.astype(mv.dtype), mv)
    return out.reshape(B, S, N_MEM_HEADS * Dh)


def setup_inputs(seed: int = 0) -> dict:
    key = jax.random.key(seed)
    ks = iter(jax.random.split(key, 32))
    f32 = jnp.float32

    def w(shape, fan_in):
        return jax.random.normal(next(ks), shape, f32) * (fan_in ** -0.5)

    def gain(shape):
        return 1.0 + 0.02 * jax.random.normal(next(ks), shape, f32)

    def bias(shape):
        return 0.02 * jax.random.normal(next(ks), shape, f32)

    L = DEPTH
    x = jax.random.normal(next(ks), (BATCH, SEQ, D_MODEL), f32)
    mem = jax.random.normal(next(ks), (BATCH, MEM_LEN, D_MODEL), f32)
    positions = jnp.broadcast_to(jnp.arange(SEQ, dtype=jnp.int32)[None, :], (BATCH, SEQ))
    return {
        "x": x,
        "mem": mem,
        "positions": positions,
        "ffn1_norm": gain((L, D_MODEL)),
        "ffn1_w1": w((L, D_MODEL, D_FF), D_MODEL),
        "ffn1_w3": w((L, D_MODEL, D_FF), D_MODEL),
        "ffn1_w2": w((L, D_FF, D_MODEL), D_FF),
        "mix_norm": gain((L, D_MODEL)),
        "w_in": w((L, D_MODEL, D_IN), D_MODEL),
        "conv_w": w((L, CONV_WIDTH, CONV_CH), CONV_WIDTH),
        "conv_b": bias((L, CONV_CH)),
        "conv_ln_g": gain((L, CONV_CH)),
        "conv_ln_b": bias((L, CONV_CH)),
        "swa_q_norm": gain((L, HEAD_DIM)),
        "swa_k_norm": gain((L, HEAD_DIM)),
        "swa_sinks": jax.random.normal(next(ks), (L, N_Q_HEADS), f32),
        "mem_norm": gain((L, D_MODEL)),
        "w_mem_kv": w((L, D_MODEL, 2 * MEM_W), D_MODEL),
        "mem_q_norm": gain((L, HEAD_DIM)),
        "mem_k_norm": gain((L, HEAD_DIM)),
        "w_out": w((L, D_MIX, D_MODEL), D_MIX),
        "ffn2_norm": gain((L, D_MODEL)),
        "ffn2_w1": w((L, D_MODEL, D_FF), D_MODEL),
        "ffn2_w3": w((L, D_MODEL, D_FF), D_MODEL),
        "ffn2_w2": w((L, D_FF, D_MODEL), D_FF),
        "final_norm": gain((L, D_MODEL)),
    }


def reference(x, mem, positions, ffn1_norm, ffn1_w1, ffn1_w3, ffn1_w2, mix_norm, w_in,
              conv_w, conv_b, conv_ln_g, conv_ln_b, swa_q_norm, swa_k_norm, swa_sinks,
              mem_norm, w_mem_kv, mem_q_norm, mem_k_norm, w_out,
              ffn2_norm, ffn2_w1, ffn2_w3, ffn2_w2, final_norm):
    B, S, _ = x.shape
    cos, sin = rope_tables(positions)
    splits = [2 * CONV_CH, 2 * CONV_CH + SWA_Q, 2 * CONV_CH + SWA_Q + SWA_KV,
              2 * CONV_CH + SWA_Q + 2 * SWA_KV]
    for l in range(DEPTH):
        h = x + 0.5 * swiglu(rms_norm(x, ffn1_norm[l]), ffn1_w1[l], ffn1_w3[l], ffn1_w2[l])

        n = rms_norm(h, mix_norm[l])
        p = n @ w_in[l]
        u_conv, q, k, v, q_mem = jnp.split(p, splits, axis=-1)

        y_conv = conv_module(u_conv, conv_w[l], conv_b[l], conv_ln_g[l], conv_ln_b[l])

        q = apply_rope(rms_norm(q.reshape(B, S, N_Q_HEADS, HEAD_DIM), swa_q_norm[l]), cos, sin)
        k = apply_rope(rms_norm(k.reshape(B, S, N_KV_HEADS, HEAD_DIM), swa_k_norm[l]), cos, sin)
        v = v.reshape(B, S, N_KV_HEADS, HEAD_DIM)
        y_swa = sliding_window_attention(q, k, v, swa_sinks[l])

        mkv = rms_norm(mem, mem_norm[l]) @ w_mem_kv[l]
        mk, mv = jnp.split(mkv, 2, axis=-1)
        mk = rms_norm(mk.reshape(B, MEM_LEN, N_MEM_HEADS, HEAD_DIM), mem_k_norm[l])
        mv = mv.reshape(B, MEM_LEN, N_MEM_HEADS, HEAD_DIM)
        qm = rms_norm(q_mem.reshape(B, S, N_MEM_HEADS, HEAD_DIM), mem_q_norm[l])
        y_mem = memory_attention(qm, mk, mv)

        y = jnp.concatenate([y_conv, y_swa, y_mem], axis=-1)
        h = h + y @ w_out[l]

        h = h + 0.5 * swiglu(rms_norm(h, ffn2_norm[l]), ffn2_w1[l], ffn2_w3[l], ffn2_w2[l])

        x = rms_norm(h, final_norm[l])
    return x
```

```python
import math
from contextlib import ExitStack

import numpy as np
import concourse.bass as bass
import concourse.mybir as mybir
from concourse.bass_utils import run_bass_kernel_spmd

F32 = mybir.dt.float32
BF16 = mybir.dt.bfloat16
I32 = mybir.dt.int32
AF = mybir.ActivationFunctionType
ALU = mybir.AluOpType

D = 1024
KC = 8
DFF = 2816
NG = 11
DIN = 1664
MEM = 256
EPS = 1e-6
NEG = -30000.0
ENGS = ("pe", "act", "dve", "pool", "sp")

C_FFN1, C_MIX, C_FFN2, C_FIN = 0, 8, 16, 24
C_CW = 32
C_CB, C_LG, C_LB = 125, 128, 131
C_QN, C_KN, C_MQN, C_MKN = 134, 135, 136, 137
C_SINK = 138
C_MEMN = 141
NCOL = 149

CB_ID, CB_ONES, CB_BLK, CB_ROT, CB_ONEG0, CB_ONEG1 = 0, 128, 256, 384, 512, 640
CB_MC, CB_MP, CB_MPF = 768, 1152, 1536
NCB = 1920


def _isz(dt):
    return {F32: 4, BF16: 2, I32: 4}[dt]


FUSE_WAITS = True


class Op:
    __slots__ = ("eng", "fn", "deps", "ddeps", "edeps", "eddeps", "dma", "sig", "waits", "fuse")


class Prog:
    def __init__(self):
        self.ops = []
        self.state = {}
        self.dcnt = {}

    def _gran(self, r):
        if isinstance(r, str):
            return [r]
        name = r.tensor.name
        isz = _isz(r.dtype)
        dims = list(r.ap)
        pstride = dims[0][0]
        off = r.offset % pstride if pstride else r.offset
        free = dims[1:]
        if not free:
            free = [(1, 1)]
        inner = free[-1]
        outer = free[:-1]
        span = (inner[0] * (inner[1] - 1) + 1) if inner[0] != 0 else 1
        res = set()
        idxs = [0] * len(outer)
        while True:
            o = off + sum(i * s for i, (s, _) in zip(idxs, outer))
            lo = (o * isz) // 64
            hi = ((o + span) * isz - 1) // 64
            for g in range(lo, hi + 1):
                res.add((name, g))
            k = len(outer) - 1
            while k >= 0:
                idxs[k] += 1
                if idxs[k] < outer[k][1]:
                    break
                idxs[k] = 0
                k -= 1
            if k < 0:
                break
        return res

    def add(self, eng, fn, reads=(), writes=(), dma=None, early=()):
        idx = len(self.ops)
        deps = set()
        isdma = dma is not None
        e_all = set()
        for r in early:
            for g in self._gran(r):
                st = self.state.get(g)
                if st is not None and st[0] is not None:
                    e_all.add(st[0])
        locks = set()
        for r in list(reads) + list(writes):
            if not isinstance(r, str) and r.tensor.name == "ps":
                for (_, g) in self._gran(r):
                    locks.add("pslock%d" % (g // 32))
        writes = list(writes) + sorted(locks)
        for r in reads:
            for g in self._gran(r):
                st = self.state.get(g)
                if st is None:
                    st = self.state[g] = [None, {}, []]
                if st[0] is not None:
                    deps.add(st[0])
                if isdma:
                    st[2].append(idx)
                else:
                    st[1][eng] = idx
        for w in writes:
            for g in self._gran(w):
                st = self.state.get(g)
                if st is None:
                    st = self.state[g] = [None, {}, []]
                if st[0] is not None:
                    deps.add(st[0])
                deps.update(st[1].values())
                deps.update(st[2])
                st[0] = idx
                st[1] = {}
                st[2] = []
        deps.discard(idx)
        op = Op()
        cdeps = set()
        ddeps = {}
        for d in deps:
            dop = self.ops[d]
            if dop.dma is not None:
                ddeps[dop.dma] = self.dcnt[dop.dma]
            else:
                cdeps.add(d)
        op.eng, op.fn, op.deps, op.ddeps, op.dma = eng, fn, cdeps, ddeps, dma
        op.edeps = {d for d in e_all if self.ops[d].dma is None}
        op.eddeps = {self.ops[d].dma: self.dcnt[self.ops[d].dma] for d in e_all if self.ops[d].dma is not None}
        op.sig = None
        if dma is not None:
            self.dcnt[dma] = self.dcnt.get(dma, 0) + 16
            op.sig = ("dma:" + dma, self.dcnt[dma])
        self.ops.append(op)
        return idx

    def emit(self, nc, es):
        ops = self.ops
        needed = set()
        for op in ops:
            for d in op.deps:
                dop = ops[d]
                if dop.eng == "pe" and op.eng == "pe" and dop.dma is None and op.dma is None:
                    continue
                needed.add(d)
        cnt = {e: 0 for e in ENGS}
        dcnt = self.dcnt
        for i, op in enumerate(ops):
            if op.dma is not None:
                pass
            elif i in needed:
                cnt[op.eng] += 1
                op.sig = (op.eng, cnt[op.eng])
            else:
                op.sig = None
        waited = {e: {} for e in ENGS}
        for op in ops:
            need = {}
            for d in op.deps:
                dop = ops[d]
                if dop.eng == "pe" and op.eng == "pe" and dop.dma is None and op.dma is None:
                    continue
                s, v = dop.sig
                if need.get(s, 0) < v:
                    need[s] = v
            for k, v in op.ddeps.items():
                need["dma:" + k] = v
            w = waited[op.eng]
            op.waits = [(s, v) for s, v in need.items() if w.get(s, 0) < v]
            for s, v in op.waits:
                w[s] = v
            op.fuse = None
            if FUSE_WAITS and op.fn is not None and op.dma is None and op.waits:
                early_s = set()
                for d in op.edeps:
                    dop = ops[d]
                    if not (dop.eng == "pe" and op.eng == "pe"):
                        early_s.add(dop.sig[0])
                for k in op.eddeps:
                    early_s.add("dma:" + k)
                cands = [x for x in op.waits if x[0] not in early_s]
                if cands:
                    op.fuse = cands[-1]
                    op.waits = [x for x in op.waits if x is not op.fuse]
        semnames = list(ENGS) + sorted({"dma:" + k for k in dcnt})
        sems = {s: es.enter_context(nc.semaphore("s_" + s.replace(":", "_"))) for s in semnames}
        per = {e: [op for op in ops if op.eng == e] for e in ENGS}
        block = es.enter_context(nc.Block())

        def run(eng, lst):
            for op in lst:
                for s, v in op.waits:
                    eng.wait_ge(sems[s], v)
                if op.fn is None:
                    continue
                inst = op.fn(eng)
                if op.fuse is not None:
                    inst._wait_ge(sems[op.fuse[0]], op.fuse[1])
                if op.sig is not None:
                    inst.then_inc(sems[op.sig[0]], 16 if op.dma is not None else 1)

        @block.tensor
        def _(e):
            run(e, per["pe"])

        @block.scalar
        def _(e):
            run(e, per["act"])

        @block.vector
        def _(e):
            run(e, per["dve"])

        @block.gpsimd
        def _(e):
            run(e, per["pool"])

        @block.sync
        def _(e):
            run(e, per["sp"])

        self.stats = {e: len(per[e]) for e in ENGS}
        self.stats["sig"] = dict(cnt)


def make_tiles(nb, per, sb=0):
    out, b = [], sb
    first = (nb - sb) % per
    while b < nb:
        n = first if (b == sb and first) else per
        out.append((b * 128, n * 128))
        b += n
    return out


def build_program(NB, HALO, DEPTH, dbg=False):
    TC = NB * 128
    NOUT = (NB - HALO) * 128
    nc = bass.Bass("TRN2", target_bir_lowering=False)
    P = Prog()

    def din(name, shape, dt=F32):
        return nc.dram_tensor(name, list(shape), dt, kind="ExternalInput").ap()

    xT = din("xT", [D, TC])
    memT = din("memT", [D, MEM])
    posr = din("posr", [128, TC], I32)
    w1d = [din("ffn1_w1", [DEPTH, D, DFF]), din("ffn2_w1", [DEPTH, D, DFF])]
    w3d = [din("ffn1_w3", [DEPTH, D, DFF]), din("ffn2_w3", [DEPTH, D, DFF])]
    w2d = [din("ffn1_w2", [DEPTH, DFF, D]), din("ffn2_w2", [DEPTH, DFF, D])]
    wind = din("w_in", [DEPTH, D, DIN])
    woutd = din("w_out", [DEPTH, D, D])
    wmemd = din("w_mem_kv", [DEPTH, D, 512])
    colsd = din("cols", [128, DEPTH, NCOL])
    cbd = din("cbf", [128, NCB])
    cfd = din("cff", [128, 2])
    outT = nc.dram_tensor("outT", [D, NOUT], F32, kind="ExternalOutput").ap()
    ropeD = nc.dram_tensor("ropeD", [2, 128, TC], F32, kind="Internal").ap()
    memhD = nc.dram_tensor("memhD", [128, KC * MEM], BF16, kind="Internal").ap()
    dbgT = None
    if dbg:
        dbgT = nc.dram_tensor("dbgT", [4, D, TC], F32, kind="ExternalOutput").ap()

    es = ExitStack()
    with es:
        def sb(name, shape, dt):
            return es.enter_context(nc.sbuf_tensor(name, list(shape), dt))

        xs = sb("xs", [128, KC, TC], F32)
        BSZ = max(KC * TC, 20480)
        Bt = sb("Bt", [128, BSZ], BF16)
        nT = Bt[:, 0:KC * TC].rearrange("p (k t) -> p k t", k=KC)
        wout = Bt[:, 0:8192].rearrange("p (k n) -> p k n", k=KC)
        diag = Bt[:, 8192:8192 + 3968].rearrange("p (j m) -> p j m", j=31)
        wmem = Bt[:, 8192:8192 + 4096].rearrange("p (k n) -> p k n", k=KC)
        ntile = Bt[:, 12288:16384].rearrange("p (k t) -> p k t", k=KC)
        yT = Bt[:, 16384:20480].rearrange("p (k t) -> p k t", k=KC)
        memn = yT[:, :, 0:256]
        ring = sb("ring", [128, 13312], BF16)
        win_b = ring[:, 0:7168].rearrange("p (k n) -> p k n", k=KC)
        win_a = ring[:, 7168:13312].rearrange("p (k n) -> p k n", k=KC)

        def wcol(kc, c0, n=128):
            if c0 < 768:
                return win_a[:, kc, c0:c0 + n]
            return win_b[:, kc, c0 - 768:c0 - 768 + n]
        cols = sb("cols_sb", [128, DEPTH, NCOL], F32)
        cb = sb("cb", [128, NCB], BF16)
        cf = sb("cf", [128, 2], F32)
        S = sb("S", [128, 4, 512], F32)
        sqb = sb("sqb", [128, 2, 512], BF16)
        gT = sb("gT", [128, 2, 2, 512], BF16)
        cs = sb("cs", [128, 2, 512], F32)
        glu = sb("glu", [128, 3, 544], BF16)
        yf = sb("yf", [128, 3, 512], F32)
        ybf = sb("ybf", [128, 3, 512], BF16)
        qT = sb("qT", [128, 3, 512], BF16)
        qn = sb("qn", [128, 2, 512], BF16)
        kTz = sb("kTz", [128, 2, 640], BF16)
        Vz = sb("Vz", [128, 5, 2, 128], BF16)
        qmT = sb("qmT", [128, 2, 512], BF16)
        mkTz = sb("mkTz", [128, 4, 256], BF16)
        mvz = sb("mvz", [128, 2, 4, 128], BF16)
        EE = sb("EE", [128, 2, 4, 384], BF16)
        EEflat = EE[:, :, :, :].rearrange("p a b c -> p (a b c)")
        Emem = EEflat[:, 0:2048].rearrange("p (h x) -> p h x", h=2)
        esink = sb("esink", [128, 3], F32)
        ps = es.enter_context(nc.psum_tensor("ps", [128, 8, 512], F32))

        ident = cb[:, CB_ID:CB_ID + 128]
        ones = cb[:, CB_ONES:CB_ONES + 128]
        blk = cb[:, CB_BLK:CB_BLK + 128]
        rotm = cb[:, CB_ROT:CB_ROT + 128]
        oneg = [cb[:, CB_ONEG0:CB_ONEG0 + 128], cb[:, CB_ONEG1:CB_ONEG1 + 128]]
        maskC = cb[:, CB_MC:CB_MC + 384]
        maskP = cb[:, CB_MP:CB_MP + 384]
        maskPF = cb[:, CB_MPF:CB_MPF + 384]

        rr = {"h": 0, "f": 0}

        def ps_half(T=256):
            return ps_full(T)

        def ps_full(T=512):
            f = rr["f"]
            rr["f"] = (f + 1) % 6
            return ps[:, f, 0:T]

        srr = {"i": 0, "q": 0}

        def s_buf(T):
            i = srr["i"]
            srr["i"] = (i + 1) % 4
            return S[:, i, 0:T]

        def sq_buf(T):
            i = srr["q"]
            srr["q"] = (i + 1) % 2
            return sqb[:, i, 0:T]

        def mm(out, lhsT, rhs, start, stop):
            P.add("pe", lambda e: e.matmul(out, lhsT, rhs, start=start, stop=stop),
                  reads=[lhsT, rhs], writes=[out], early=[lhsT])

        def act(out, in_, func, bias=None, scale=None, extra=()):
            kw = {}
            if bias is not None:
                kw["bias"] = bias
            if scale is not None:
                kw["scale"] = scale
            rd = [in_] + list(extra)
            P.add("act", lambda e: e.activation(out, in_, func, **kw), reads=rd, writes=[out])

        def tt(out, a, b, op, eng="dve"):
            P.add(eng, lambda e: e.tensor_tensor(out, a, b, op), reads=[a, b], writes=[out])

        def ts(out, a, s1, op0, s2=None, op1=None, eng="dve"):
            rd = [a] + [s for s in (s1, s2) if not isinstance(s, (int, float, type(None)))]
            if op1 is None:
                P.add(eng, lambda e: e.tensor_scalar(out, a, s1, None, op0), reads=rd, writes=[out])
            else:
                P.add(eng, lambda e: e.tensor_scalar(out, a, s1, s2, op0, op1), reads=rd, writes=[out])

        def stt(out, a, s, b, op0, op1):
            rd = [a, b] + ([s] if not isinstance(s, (int, float)) else [])
            P.add("dve", lambda e: e.scalar_tensor_tensor(out, a, s, b, op0, op1), reads=rd, writes=[out])

        def cp(out, in_, eng="dve"):
            if eng == "act":
                P.add("act", lambda e: e.activation(out, in_, AF.Identity), reads=[in_], writes=[out])
            else:
                P.add(eng, lambda e: e.tensor_copy(out, in_), reads=[in_], writes=[out])

        swq = {"out": [], "tot": 0}
        SW_BUDGET = 10000

        def _rows(ap):
            n = 1
            for (_, c) in list(ap.ap)[:-1]:
                n *= c
            return n

        def dma(q, out, in_, key, reads=None, writes=None):
            rd = [in_] if reads is None else reads
            wr = [out] if writes is None else writes
            idx = P.add(q, lambda e: e.dma_start(out=out, in_=in_), reads=rd, writes=wr, dma=key)
            if q == "pool":
                nd = max(_rows(out), _rows(in_))
                op = P.ops[idx]
                while swq["out"] and swq["tot"] + nd > SW_BUDGET:
                    k_, v_, n_ = swq["out"].pop(0)
                    swq["tot"] -= n_
                    if op.ddeps.get(k_, 0) < v_:
                        op.ddeps[k_] = v_
                swq["out"].append((key, op.sig[1], nd))
                swq["tot"] += nd

        def memset(ap, val, eng="pool"):
            P.add(eng, lambda e: e.memset(ap, val), writes=[ap])

        def rstd_from(stat_ps, scale, T, to_sbuf=False):
            act(stat_ps, stat_ps, AF.Ln, bias=epsc, scale=scale, extra=[epsc])
            dst = s_buf(T) if to_sbuf else stat_ps
            act(dst, stat_ps, AF.Exp, scale=-0.5)
            return dst

        nsq = {"i": 0}

        def nsq_buf(T):
            pool_ = [sqb[:, 0, 0:T], sqb[:, 1, 0:T], ybf[:, 0, 0:T], ybf[:, 1, 0:T], ybf[:, 2, 0:T],
                     qn[:, 0, 0:T], qn[:, 1, 0:T], qmT[:, 0, 0:T]]
            i = nsq["i"]
            nsq["i"] = (i + 1) % 8
            return pool_[i]

        def rms_rstd(chunks, T, lhs=None, wide=True):
            st = ps_full(T)
            n = len(chunks)
            for i, x in enumerate(chunks):
                q = nsq_buf(T) if wide else sq_buf(T)
                act(q, x, AF.Square)
                mm(st, ones if lhs is None else lhs, q, i == 0, i == n - 1)
            return rstd_from(st, 1.0 / D, T)

        epsc = None
        dma("pool", cb[:, :], cbd[:, :], "constp", reads=["d:cbf"])
        dma("sp", cf[:, :], cfd[:, :], "const", reads=["d:cff"])
        dma("sp", cols[:, :, :], colsd[:, :, :], "const", reads=["d:cols"])
        epsc_t = sb("epsc", [128, 2], F32)
        memset(epsc_t[:, 0:1], EPS)
        memset(epsc_t[:, 1:2], math.pi / 2)
        epsc = epsc_t[:, 0:1]
        halfpi = epsc_t[:, 1:2]
        memset(kTz[:, :, :], 0.0)
        memset(Vz[:, :, :, :], 0.0)
        memset(mkTz[:, :, :], 0.0)
        memset(mvz[:, :, :, :], 0.0)
        memset(glu[:, :, :], 0.0)
        for kc in range(KC):
            dma("act", xs[:, kc, :], xT[kc * 128:(kc + 1) * 128, :], "xin", reads=["d:x"])

        wg = [(l, w, g) for l in range(DEPTH) for w in (0, 1) for g in range(NG)]
        wstate = {"next": 0}

        def slot_views(s):
            b0 = 0 if s == 0 else 7168
            w1g = ring[:, b0:b0 + 2048].rearrange("p (k f) -> p k f", k=KC)
            w3g = ring[:, b0 + 2048:b0 + 4096].rearrange("p (k f) -> p k f", k=KC)
            w2g = ring[:, b0 + 4096:b0 + 6144].rearrange("p (c n) -> p c n", c=2)
            return w1g, w3g, w2g

        def load_next_group(force=False):
            i = wstate["next"]
            if i < len(wg) and wg[i][1] == 1 and wg[i][2] < 2 and not force:
                return
            if i >= len(wg):
                return
            wstate["next"] = i + 1
            l, w, g = wg[i]
            s = i % 2
            w1g, w3g, w2g = slot_views(s)
            f0 = g * 256
            dma("pool", w1g, w1d[w][l, :, f0:f0 + 256].rearrange("(k p) f -> p k f", p=128), "ring%d" % s, reads=["d:w"])
            dma("pool", w3g, w3d[w][l, :, f0:f0 + 256].rearrange("(k p) f -> p k f", p=128), "ring%d" % s, reads=["d:w"])
            dma("pool", w2g, w2d[w][l, f0:f0 + 256, :].rearrange("(c p) n -> p c n", p=128), "ring%d" % s, reads=["d:w"])

        load_next_group()
        load_next_group()

        TWO_PI_HI = 6.28125
        TWO_PI_LO = 2.0 * math.pi - 6.28125
        for ri, (t0, T) in enumerate(make_tiles(NB, 4)):
            if ri % 2 == 0:
                B0, B1, B2, B3 = S[:, 0, 0:T], S[:, 1, 0:T], S[:, 2, 0:T], S[:, 3, 0:T]
            else:
                B0, B1, B2, B3 = yf[:, 0, 0:T], yf[:, 1, 0:T], yf[:, 2, 0:T], cs[:, 0, 0:T]
            pi_t = B0.bitcast(I32)
            dma("pool", pi_t, posr[:, t0:t0 + T], "ropein%d" % (ri % 2), reads=["d:pos"])
            ang = B1
            ts(ang, pi_t, cf[:, 0:1], ALU.mult)
            kf = B2
            ts(kf, ang, 1.0 / (2.0 * math.pi), ALU.mult)
            k2 = B3
            ts(k2, kf, 12582912.0, ALU.add)
            ts(kf, k2, -12582912.0, ALU.add)
            r = B3
            stt(r, kf, -TWO_PI_HI, ang, ALU.mult, ALU.add)
            stt(r, kf, -TWO_PI_LO, r, ALU.mult, ALU.add)
            ts(r, r, 3.1415925, ALU.min, -3.1415925, ALU.max)
            ng = B2
            ts(ng, r, -1.0, ALU.mult)
            ab = B2
            tt(ab, ng, r, ALU.max)
            sn = B1
            act(sn, r, AF.Sin)
            co = B0
            act(co, ab, AF.Sin, bias=halfpi, scale=-1.0, extra=[halfpi])
            dma("sp", ropeD[0, :, t0:t0 + T], co, "ropeout%d" % (ri % 2), writes=["d:ropeC%d" % ri])
            dma("sp", ropeD[1, :, t0:t0 + T], sn, "ropeout%d" % (ri % 2), writes=["d:ropeS%d" % ri])

        memh_sb = Bt[:, 0:KC * MEM].rearrange("p (k t) -> p k t", k=KC)
        for h in range(2):
            stg = S[:, 0:2, :].rearrange("p a (k t) -> p (a k) t", t=128)
            for kc in range(KC):
                dma("sp", stg[:, kc, :], memT[kc * 128:(kc + 1) * 128, h * 128:(h + 1) * 128], "memin", reads=["d:mem"])
            st = ps_full(128)
            for kc in range(KC):
                q = sq_buf(128)
                act(q, stg[:, kc, :], AF.Square)
                mm(st, ones, q, kc == 0, kc == KC - 1)
            lnv = S[:, 2, 0:128]
            act(lnv, st, AF.Ln, bias=epsc, scale=1.0 / D, extra=[epsc])
            act(st, lnv, AF.Exp, scale=-0.5)
            for kc in range(KC):
                tt(memh_sb[:, kc, h * 128:(h + 1) * 128], stg[:, kc, :], st, ALU.mult)

        dma("sp", memhD, memh_sb.rearrange("p k t -> p (k t)"), "memhout", writes=["d:memh"])

        def start_block(l):
            return max(0, HALO - (DEPTH - l))

        def norm_to(dst_fn, src_fn, T, gcol0, l, out_f32=False):
            chunks = [src_fn(kc) for kc in range(KC)]
            rs = rms_rstd(chunks, T)
            for kc in range(KC):
                stt(dst_fn(kc), chunks[kc], cols[:, l, gcol0 + kc:gcol0 + kc + 1], rs, ALU.mult, ALU.mult)

        def ffn_phase(l, w):
            ftiles = make_tiles(NB, 4, start_block(l) if w == 0 else start_block(l + 1))
            gbase = C_FFN1 if w == 0 else C_FFN2
            for (t0, T) in ftiles:
                norm_to(lambda kc: nT[:, kc, t0:t0 + T], lambda kc: xs[:, kc, t0:t0 + T], T, gbase, l)
            its = [(g, ti) for g in range(NG) for ti in range(len(ftiles))]
            gidx0 = (l * 2 + w) * NG
            par = {"o": 0}

            def emit_ab(k, c):
                g, ti = its[k]
                t0, T = ftiles[ti]
                s = (gidx0 + g) % 2
                w1g, w3g, _ = slot_views(s)
                a_ps = ps[:, c, 0:T]
                b_ps = ps[:, 2 + c, 0:T]
                for kc in range(KC):
                    mm(a_ps, w1g[:, kc, c * 128:(c + 1) * 128], nT[:, kc, t0:t0 + T], kc == 0, kc == KC - 1)
                for kc in range(KC):
                    mm(b_ps, w3g[:, kc, c * 128:(c + 1) * 128], nT[:, kc, t0:t0 + T], kc == 0, kc == KC - 1)
                sa = S[:, c, 0:T]
                act(sa, a_ps, AF.Silu)
                tt(gT[:, k % 2, c, 0:T], sa, b_ps, ALU.mult)

            def emit_out(k):
                g, ti = its[k]
                t0, T = ftiles[ti]
                s = (gidx0 + g) % 2
                _, _, w2g = slot_views(s)
                for dc in range(KC):
                    o = ps[:, 4 + par["o"], 0:T]
                    par["o"] = (par["o"] + 1) % 4
                    for c in range(2):
                        mm(o, w2g[:, c, dc * 128:(dc + 1) * 128], gT[:, k % 2, c, 0:T], c == 0, c == 1)
                    stt(xs[:, dc, t0:t0 + T], o, 0.5, xs[:, dc, t0:t0 + T], ALU.mult, ALU.add)
                if ti == len(ftiles) - 1:
                    load_next_group()

            n = len(its)
            emit_ab(0, 0)
            emit_ab(0, 1)
            for k in range(n):
                if k + 1 < n:
                    emit_ab(k + 1, 0)
                emit_out(k)
                if k + 1 < n:
                    emit_ab(k + 1, 1)

        def qk_norm(p_ps, T):
            q = sq_buf(T)
            act(q, p_ps, AF.Square)
            st = ps_full(T)
            mm(st, blk, q, True, True)
            return rstd_from(st, 1.0 / 64.0, T, to_sbuf=True)

        def mixer_phase(l):
            TM = 4
            dma("pool", wmem, wmemd[l].rearrange("(k p) n -> p k n", p=128), "wmem", reads=["d:w"])
            dma("pool", wout[:, 0:3, :], woutd[l, 0:384, :].rearrange("(k p) n -> p k n", p=128), "wout", reads=["d:w"])
            for j in range(3):
                for g in range(2):
                    r0 = 384 + 64 * (3 * g + j)
                    dma("pool", wout[64 * g:64 * g + 64, 3 + j, :], woutd[l, r0:r0 + 64, :], "wout", reads=["d:w"])
            dma("pool", wout[:, 6:8, :], woutd[l, 768:1024, :].rearrange("(k p) n -> p k n", p=128), "wout", reads=["d:w"])
            dma("sp", memn, memhD.rearrange("p (k t) -> p k t", k=KC), "memh", reads=["d:memh"])
            act(esink[:, :], cols[:, l, C_SINK:C_SINK + 3], AF.Exp)
            for kc in range(KC):
                ts(memn[:, kc, :], memn[:, kc, :], cols[:, l, C_MEMN + kc:C_MEMN + kc + 1], ALU.mult)
            def mem_kv_prep():
                for m_ in range(2):
                    pk = ps_full(256)
                    for kc in range(KC):
                        mm(pk, wmem[:, kc, m_ * 128:(m_ + 1) * 128], memn[:, kc, :], kc == 0, kc == KC - 1)
                    rs = qk_norm(pk, 256)
                    tmp = sq_buf(256)
                    stt(tmp, pk, cols[:, l, C_MKN:C_MKN + 1], rs, ALU.mult, ALU.mult)
                    for hh in range(2):
                        cp(mkTz[64 * hh:64 * hh + 64, 2 * m_ + hh, :], tmp[64 * hh:64 * hh + 64, :])
                for mc in range(2):
                    pv = ps_full(256)
                    for kc in range(KC):
                        mm(pv, memn[:, kc, mc * 128:(mc + 1) * 128], wmem[:, kc, 256:512], kc == 0, kc == KC - 1)
                    for h in range(4):
                        cp(mvz[:, mc, h, (h % 2) * 64:(h % 2) * 64 + 64], pv[:, h * 64:(h + 1) * 64], eng="act")


            mtiles = make_tiles(NB, TM, start_block(l))
            for ti, (t0, T) in enumerate(mtiles):
                nblk = T // 128
                rti = [i_ for i_, (a_, w_) in enumerate(make_tiles(NB, 4)) if a_ <= t0 < a_ + w_][0]
                dma("sp", cs[:, 0, 0:T], ropeD[0, :, t0:t0 + T], "rope", reads=["d:ropeC%d" % rti])
                dma("sp", cs[:, 1, 0:T], ropeD[1, :, t0:t0 + T], "rope", reads=["d:ropeS%d" % rti])
                cosT = cs[:, 0, 0:T]
                sinT = cs[:, 1, 0:T]
                norm_to(lambda kc: ntile[:, kc, 0:T], lambda kc: xs[:, kc, t0:t0 + T], T, C_MIX, l)

                def proj(col0):
                    o = ps_full(T)
                    for kc in range(KC):
                        mm(o, wcol(kc, col0), ntile[:, kc, 0:T], kc == 0, kc == KC - 1)
                    return o

                for c in range(3):
                    a_ps = proj(c * 128)
                    g_ps = proj(384 + c * 128)
                    sg = s_buf(T)
                    act(sg, g_ps, AF.Sigmoid)
                    tt(glu[:, c, 32:32 + T], sg, a_ps, ALU.mult)
                if HALO > 0:
                    hb0 = (HALO - 1) * 128
                    if t0 <= hb0 < t0 + T:
                        o_ = 32 + hb0 - t0
                        ts(glu[:, :, o_:o_ + 128], glu[:, :, o_:o_ + 128], cf[:, 1:2], ALU.mult)

                def build_diag(c):
                    for j in range(31):
                        ts(diag[:, j, :], ident, cols[:, l, C_CW + c * 31 + j:C_CW + c * 31 + j + 1], ALU.mult)

                if ti > 0:
                    build_diag(0)

                st1 = {}

                def s1(j):
                    col0 = 768 + j * 128 if j < 3 else 1152
                    p_ps = proj(col0)
                    st1[j] = (p_ps, qk_norm(p_ps, T))

                def s2(j):
                    p_ps, rs = st1.pop(j)
                    gcol = C_QN if j < 3 else C_KN
                    qb = qn[:, j % 2, 0:T]
                    stt(qb, p_ps, cols[:, l, gcol:gcol + 1], rs, ALU.mult, ALU.mult)
                    rp = ps_full(T)
                    mm(rp, rotm, qb, True, True)
                    t1 = s_buf(T)
                    tt(t1, qb, cosT, ALU.mult)
                    t2 = s_buf(T)
                    tt(t2, rp, sinT, ALU.mult)
                    if j < 3:
                        tt(qT[:, j, 0:T], t1, t2, ALU.add)
                    else:
                        tt(kTz[0:64, 0, 128:128 + T], t1[0:64, :], t2[0:64, :], ALU.add)
                        tt(kTz[64:128, 1, 128:128 + T], t1[64:128, :], t2[64:128, :], ALU.add)

                s1(0)
                for j in range(1, 4):
                    s1(j)
                    s2(j - 1)
                pvb = ps_full(T)
                for bi in range(nblk):
                    for kc in range(KC):
                        mm(pvb[:, bi * 128:(bi + 1) * 128], ntile[:, kc, bi * 128:(bi + 1) * 128], wcol(kc, 1280),
                           kc == 0, kc == KC - 1)
                s2(3)
                pv3 = pvb.rearrange("p (b f) -> p b f", f=128)
                cp(Vz[:, 1:1 + nblk, 0, 0:64], pv3[:, :, 0:64], eng="act")
                cp(Vz[:, 1:1 + nblk, 1, 64:128], pv3[:, :, 64:128], eng="act")
                qm_ps = [proj(1408 + m_ * 128) for m_ in range(2)]
                for m_ in range(2):
                    rs = qk_norm(qm_ps[m_], T)
                    stt(qmT[:, m_, 0:T], qm_ps[m_], cols[:, l, C_MQN:C_MQN + 1], rs, ALU.mult, ALU.mult)
                if ti == 0:
                    mem_kv_prep()
                    build_diag(0)

                sum_ps = ps[:, 6, 0:T]
                sq_ps = ps[:, 7, 0:T]

                def conv_chunk(c):
                    y_ps = ps_full(T)
                    for j in range(31):
                        mm(y_ps, diag[:, j, :], glu[:, c, 2 + j:2 + j + T], j == 0, j == 30)
                    bcol = cols[:, l, C_CB + c:C_CB + c + 1]
                    act(ybf[:, c, 0:T], y_ps, AF.Identity, bias=bcol, extra=[bcol])
                    q = sq_buf(T)
                    act(q, y_ps, AF.Square, bias=bcol, extra=[bcol])
                    ts(yf[:, c, 0:T], y_ps, bcol, ALU.add)
                    if c < 2:
                        build_diag(c + 1)

                    def stats():
                        mm(sum_ps, ones, ybf[:, c, 0:T], c == 0, c == 2)
                        mm(sq_ps, ones, q, c == 0, c == 2)
                    return stats

                def swa_S(bi, half):
                    Esw = EE[:, half]
                    blk_id = (t0 // 128) + bi
                    mprev = maskPF if blk_id == HALO else maskP
                    for g in range(2):
                        for pc in range(2):
                            s_ps = ps_full(384)
                            kk = kTz[:, g, (bi + pc) * 128:(bi + pc + 1) * 128]
                            mm(s_ps.rearrange("p (j t) -> p j t", j=3), kk, qT[:, :, bi * 128:(bi + 1) * 128], True, False)
                            mm(s_ps, ident, maskC if pc == 1 else mprev, False, True)
                            act(Esw[:, g * 2 + pc, :], s_ps, AF.Exp, scale=0.125)

                def swa_PV(bi, half):
                    Esw = EE[:, half]
                    o_ps = ps_full(384)
                    d_ps = ps_full(384)
                    n_ = 0
                    for g in range(2):
                        for pc in range(2):
                            mm(o_ps, Vz[:, bi + pc, g, :], Esw[:, g * 2 + pc, :], n_ == 0, n_ == 3)
                            n_ += 1
                    n_ = 0
                    for g in range(2):
                        for pc in range(2):
                            mm(d_ps, oneg[g], Esw[:, g * 2 + pc, :], n_ == 0, n_ == 3)
                            n_ += 1
                    lnd = s_buf(384)
                    for j in range(3):
                        act(lnd[:, j * 128:(j + 1) * 128], d_ps[:, j * 128:(j + 1) * 128], AF.Ln,
                            bias=esink[:, j:j + 1], extra=[esink[:, j:j + 1]])
                    rec = s_buf(384)
                    act(rec, lnd, AF.Exp, scale=-1.0)
                    tt(yT[:, 3:6, bi * 128:(bi + 1) * 128], o_ps.rearrange("p (j t) -> p j t", j=3),
                       rec.rearrange("p (j t) -> p j t", j=3), ALU.mult)

                def swa_group(bis):
                    for i_, bi in enumerate(bis):
                        swa_S(bi, i_ % 2)
                        if i_ > 0:
                            swa_PV(bis[i_ - 1], (i_ - 1) % 2)
                    if bis:
                        swa_PV(bis[-1], (len(bis) - 1) % 2)

                def mem_attn():
                    for pr in range(2):
                        o_ps = ps_full(T)
                        d_ps = ps_full(T)
                        for hh in range(2):
                            h = 2 * pr + hh
                            E = Emem[:, hh].rearrange("p (m t) -> p m t", m=2)
                            for mc in range(2):
                                s_ps = ps_full(T)
                                mm(s_ps, mkTz[:, h, mc * 128:(mc + 1) * 128], qmT[:, pr, 0:T], True, True)
                                act(E[:, mc, 0:T], s_ps, AF.Exp, scale=0.125)
                        for hh in range(2):
                            h = 2 * pr + hh
                            E = Emem[:, hh].rearrange("p (m t) -> p m t", m=2)
                            for mc in range(2):
                                mm(o_ps, mvz[:, mc, h, :], E[:, mc, 0:T], (hh == 0 and mc == 0), (hh == 1 and mc == 1))
                        for hh in range(2):
                            E = Emem[:, hh].rearrange("p (m t) -> p m t", m=2)
                            for mc in range(2):
                                mm(d_ps, oneg[hh], E[:, mc, 0:T], (hh == 0 and mc == 0), (hh == 1 and mc == 1))
                        act(d_ps, d_ps, AF.Ln)
                        rec = s_buf(T)
                        act(rec, d_ps, AF.Exp, scale=-1.0)
                        tt(yT[:, 6 + pr, 0:T], o_ps, rec, ALU.mult)


                def ln_chain():
                    mean = s_buf(T)
                    ts(mean, sum_ps, 1.0 / 384.0, ALU.mult)
                    m2 = s_buf(T)
                    tt(m2, mean, mean, ALU.mult)
                    var = s_buf(T)
                    stt(var, sq_ps, 1.0 / 384.0, m2, ALU.mult, ALU.subtract)
                    ts(var, var, 0.0, ALU.max)
                    act(var, var, AF.Ln, bias=epsc, extra=[epsc])
                    act(sq_ps, var, AF.Exp, scale=-0.5)
                    for c in range(3):
                        tt(yf[:, c, 0:T], yf[:, c, 0:T], mean, ALU.subtract)
                        tt(yf[:, c, 0:T], yf[:, c, 0:T], sq_ps, ALU.mult)
                        gcol_ = cols[:, l, C_LG + c:C_LG + c + 1]
                        bcol_ = cols[:, l, C_LB + c:C_LB + c + 1]
                        act(yT[:, c, 0:T], yf[:, c, 0:T], AF.Silu, bias=bcol_, scale=gcol_, extra=[gcol_, bcol_])


                per = (nblk + 1) // 2
                st0 = conv_chunk(0)
                swa_group(list(range(0, per)))
                st0()
                st1 = conv_chunk(1)
                swa_group(list(range(per, nblk)))
                st1()
                conv_chunk(2)()
                cp(kTz[:, :, 0:128], kTz[:, :, T:T + 128], eng="pool")
                cp(Vz[:, 0, :, :], Vz[:, nblk, :, :], eng="pool")
                cp(glu[:, :, 0:32], glu[:, :, T:T + 32], eng="pool")
                mem_attn()
                ln_chain()

                for dc in range(KC):
                    o = ps_full(T)
                    for yc in range(KC):
                        mm(o, wout[:, yc, dc * 128:(dc + 1) * 128], yT[:, yc, 0:T], yc == 0, yc == KC - 1)
                    tt(xs[:, dc, t0:t0 + T], o, xs[:, dc, t0:t0 + T], ALU.add)

        outkeys = []

        def final_norm(l, last):
            for (t0, T) in make_tiles(NB, 4, start_block(l + 1)):
                norm_to(lambda kc: xs[:, kc, t0:t0 + T], lambda kc: xs[:, kc, t0:t0 + T], T, C_FIN, l)
                if last:
                    o0 = t0 - HALO * 128
                    for kc in range(KC):
                        k_ = "d:out%d_%d" % (kc, t0)
                        outkeys.append(k_)
                        dma("sp", outT[kc * 128:(kc + 1) * 128, o0:o0 + T], xs[:, kc, t0:t0 + T], "out", writes=[k_])

        def dbg_dump(i):
            if dbg:
                for kc in range(KC):
                    dma("sp", dbgT[i, kc * 128:(kc + 1) * 128, :], xs[:, kc, :], "dbg%d_%d" % (i, kc), writes=["d:dbg%d_%d" % (i, kc)])

        import os
        KST = int(os.environ.get("KSTAGE", "9"))
        nout = 0
        for l in range(DEPTH):
            if KST < 2:
                break
            ffn_phase(l, 0)
            if KST < 3:
                break
            if l == 0:
                dbg_dump(0)
            wv = wind[l].rearrange("(k p) n -> p k n", p=128)
            dma("pool", win_a, wv[:, :, 0:768], "ring1", reads=["d:w"])
            dma("pool", win_b, wv[:, :, 768:1664], "ring0", reads=["d:w"])
            qtmp = EEflat[:, 0:512].rearrange("p (k d) -> p k d", k=KC)

            def qblk(i):
                return win_b[:, :, i * 64:(i + 1) * 64]

            cp(qtmp, qblk(1), eng="pool")
            cp(qblk(1), qblk(3), eng="pool")
            cp(qblk(3), qblk(4), eng="pool")
            cp(qblk(4), qblk(2), eng="pool")
            cp(qblk(2), qtmp, eng="pool")
            mixer_phase(l)
            load_next_group(force=True)
            load_next_group(force=True)
            if l == 0:
                dbg_dump(1)
            if KST < 4:
                break
            ffn_phase(l, 1)
            if l == 0:
                dbg_dump(2)
            final_norm(l, l == DEPTH - 1)
            if l == 0:
                dbg_dump(3)
        allk = list(outkeys)
        if dbg:
            allk += ["d:dbg%d_%d" % (i, kc) for i in range(4) for kc in range(KC)]
        P.add("sp", None, reads=allk)
        P.emit(nc, es)
    return nc, P


def host_consts(first):
    cbf = np.zeros((128, NCB), np.float32)
    cbf[:, CB_ID:CB_ID + 128] = np.eye(128, dtype=np.float32)
    cbf[:, CB_ONES:CB_ONES + 128] = 1.0
    for b in range(2):
        cbf[64 * b:64 * b + 64, CB_BLK + 64 * b:CB_BLK + 64 * b + 64] = 1.0
    for b in range(2):
        for m in range(64):
            if m < 32:
                cbf[64 * b + m + 32, CB_ROT + 64 * b + m] = -1.0
            else:
                cbf[64 * b + m - 32, CB_ROT + 64 * b + m] = 1.0
    cbf[:, CB_ONEG0:CB_ONEG0 + 64] = 1.0
    cbf[:, CB_ONEG1 + 64:CB_ONEG1 + 128] = 1.0
    j = np.arange(128)[:, None]
    i = np.arange(128)[None, :]
    mc = np.where(j <= i, 0.0, NEG).astype(np.float32)
    mp = np.where(j > i, 0.0, NEG).astype(np.float32)
    cbf[:, CB_MC:CB_MC + 384] = np.tile(mc, (1, 3))
    cbf[:, CB_MP:CB_MP + 384] = np.tile(mp, (1, 3))
    cbf[:, CB_MPF:CB_MPF + 384] = NEG if first else np.tile(mp, (1, 3))
    cff = np.zeros((128, 2), np.float32)
    inv = (10000.0 ** (-(np.arange(0, 64, 2, dtype=np.float64)) / 64.0)).astype(np.float32)
    cff[:, 0] = np.tile(inv, 4)
    cff[:, 1] = 0.0 if first else 1.0
    return cbf, cff


def pack_cols(p, layers):
    L = len(layers)
    cols = np.zeros((128, L, NCOL), np.float32)
    for li, l in enumerate(layers):
        def dm(v):
            return np.asarray(v, np.float32).reshape(8, 128).T
        cols[:, li, C_FFN1:C_FFN1 + 8] = dm(p["ffn1_norm"][l])
        cols[:, li, C_MIX:C_MIX + 8] = dm(p["mix_norm"][l])
        cols[:, li, C_FFN2:C_FFN2 + 8] = dm(p["ffn2_norm"][l])
        cols[:, li, C_FIN:C_FIN + 8] = dm(p["final_norm"][l])
        cols[:, li, C_MEMN:C_MEMN + 8] = dm(p["mem_norm"][l])
        cw = np.asarray(p["conv_w"][l], np.float32)
        for c in range(3):
            cols[:, li, C_CW + c * 31:C_CW + (c + 1) * 31] = cw[:, c * 128:(c + 1) * 128].T
        cols[:, li, C_CB:C_CB + 3] = np.asarray(p["conv_b"][l], np.float32).reshape(3, 128).T
        cols[:, li, C_LG:C_LG + 3] = np.asarray(p["conv_ln_g"][l], np.float32).reshape(3, 128).T
        cols[:, li, C_LB:C_LB + 3] = np.asarray(p["conv_ln_b"][l], np.float32).reshape(3, 128).T
        cols[:, li, C_QN] = np.tile(np.asarray(p["swa_q_norm"][l], np.float32), 2)
        cols[:, li, C_KN] = np.tile(np.asarray(p["swa_k_norm"][l], np.float32), 2)
        cols[:, li, C_MQN] = np.tile(np.asarray(p["mem_q_norm"][l], np.float32), 2)
        cols[:, li, C_MKN] = np.tile(np.asarray(p["mem_k_norm"][l], np.float32), 2)
        sk = np.asarray(p["swa_sinks"][l], np.float32)
        for j in range(3):
            cols[0:64, li, C_SINK + j] = sk[j]
            cols[64:128, li, C_SINK + j] = sk[3 + j]
    return cols


_PROG_CACHE = {}


def get_program(NB, HALO, DEPTH, dbg=False):
    key = (NB, HALO, DEPTH, dbg)
    if key not in _PROG_CACHE:
        _PROG_CACHE[key] = build_program(NB, HALO, DEPTH, dbg)
    return _PROG_CACHE[key][0]


WNAMES = ["ffn1_w1", "ffn1_w3", "ffn1_w2", "ffn2_w1", "ffn2_w3", "ffn2_w2", "w_in", "w_out", "w_mem_kv"]


def run_layers(x, mem, positions, p, layers, HALO, n_real_blocks=16, ncore_per_seq=4, dbg=False):
    B, S_, _ = x.shape
    NB = n_real_blocks + HALO
    TC = NB * 128
    nreal = n_real_blocks * 128
    ncores = B * ncore_per_seq
    nc = get_program(NB, HALO, len(layers), dbg)
    cols = pack_cols(p, layers)
    wsl = {k: np.ascontiguousarray(np.asarray(p[k], np.float32)[layers]) for k in WNAMES}
    in_maps = []
    for c in range(ncores):
        b, qd = divmod(c, ncore_per_seq)
        start = qd * nreal - HALO * 128
        xc = np.zeros((TC, D), np.float32)
        pc = np.zeros((TC,), np.int32)
        lo = max(start, 0)
        xc[lo - start:] = x[b, lo:start + TC]
        pc[lo - start:] = positions[b, lo:start + TC]
        cbf, cff = host_consts(first=(qd == 0))
        m = {
            "xT": np.ascontiguousarray(xc.T),
            "memT": np.ascontiguousarray(np.asarray(mem[b], np.float32).T),
            "posr": np.ascontiguousarray(np.broadcast_to(pc[None, :], (128, TC))),
            "cols": cols, "cbf": cbf, "cff": cff,
        }
        m.update(wsl)
        in_maps.append(m)
    res = run_bass_kernel_spmd(nc, in_maps, core_ids=list(range(ncores)))
    out = np.zeros((B, ncore_per_seq * nreal, D), np.float32)
    for c in range(ncores):
        b, qd = divmod(c, ncore_per_seq)
        out[b, qd * nreal:(qd + 1) * nreal] = res.results[c]["outT"].T
    return out, res


FUSED = True


def kernel(**inputs):
    p = {k: np.asarray(v) for k, v in inputs.items()}
    x = np.asarray(p["x"], np.float32)
    mem = np.asarray(p["mem"], np.float32)
    pos = np.asarray(p["positions"], np.int32)
    L = p["ffn1_w1"].shape[0]
    if FUSED:
        out, _ = run_layers(x, mem, pos, p, list(range(L)), HALO=L)
        return out
    for l in range(L):
        x, _ = run_layers(x, mem, pos, p, [l], HALO=1)
    return x
```

```python
import math
from contextlib import ExitStack

import numpy as np
import concourse.bass as bass
import concourse.mybir as mybir
from concourse.bass_utils import run_bass_kernel_spmd

F32 = mybir.dt.float32
BF16 = mybir.dt.bfloat16
I32 = mybir.dt.int32
AF = mybir.ActivationFunctionType
ALU = mybir.AluOpType

D = 1024
KC = 8
DFF = 2816
NG = 11
DIN = 1664
MEM = 256
EPS = 1e-6
NEG = -30000.0
ENGS = ("pe", "act", "dve", "pool", "sp")

C_FFN1, C_MIX, C_FFN2, C_FIN = 0, 8, 16, 24
C_CW = 32
C_CB, C_LG, C_LB = 125, 128, 131
C_QN, C_KN, C_MQN, C_MKN = 134, 135, 136, 137
C_SINK = 138
C_MEMN = 141
NCOL = 149

CB_ID, CB_ONES, CB_BLK, CB_ROT, CB_ONEG0, CB_ONEG1 = 0, 128, 256, 384, 512, 640
CB_MC, CB_MP, CB_MPF = 768, 1152, 1536
NCB = 1920


def _isz(dt):
    return {F32: 4, BF16: 2, I32: 4}[dt]


FUSE_WAITS = True


class Op:
    __slots__ = ("eng", "fn", "deps", "ddeps", "edeps", "eddeps", "dma", "sig", "waits", "fuse")


class Prog:
    def __init__(self):
        self.ops = []
        self.state = {}
        self.dcnt = {}

    def _gran(self, r):
        if isinstance(r, str):
            return [r]
        name = r.tensor.name
        isz = _isz(r.dtype)
        dims = list(r.ap)
        pstride = dims[0][0]
        off = r.offset % pstride if pstride else r.offset
        free = dims[1:]
        if not free:
            free = [(1, 1)]
        inner = free[-1]
        outer = free[:-1]
        span = (inner[0] * (inner[1] - 1) + 1) if inner[0] != 0 else 1
        res = set()
        idxs = [0] * len(outer)
        while True:
            o = off + sum(i * s for i, (s, _) in zip(idxs, outer))
            lo = (o * isz) // 64
            hi = ((o + span) * isz - 1) // 64
            for g in range(lo, hi + 1):
                res.add((name, g))
            k = len(outer) - 1
            while k >= 0:
                idxs[k] += 1
                if idxs[k] < outer[k][1]:
                    break
                idxs[k] = 0
                k -= 1
            if k < 0:
                break
        return res

    def add(self, eng, fn, reads=(), writes=(), dma=None, early=()):
        idx = len(self.ops)
        deps = set()
        isdma = dma is not None
        e_all = set()
        for r in early:
            for g in self._gran(r):
                st = self.state.get(g)
                if st is not None and st[0] is not None:
                    e_all.add(st[0])
        locks = set()
        for r in list(reads) + list(writes):
            if not isinstance(r, str) and r.tensor.name == "ps":
                for (_, g) in self._gran(r):
                    locks.add("pslock%d" % (g // 32))
        writes = list(writes) + sorted(locks)
        for r in reads:
            for g in self._gran(r):
                st = self.state.get(g)
                if st is None:
                    st = self.state[g] = [None, {}, []]
                if st[0] is not None:
                    deps.add(st[0])
                if isdma:
                    st[2].append(idx)
                else:
                    st[1][eng] = idx
        for w in writes:
            for g in self._gran(w):
                st = self.state.get(g)
                if st is None:
                    st = self.state[g] = [None, {}, []]
                if st[0] is not None:
                    deps.add(st[0])
                deps.update(st[1].values())
                deps.update(st[2])
                st[0] = idx
                st[1] = {}
                st[2] = []
        deps.discard(idx)
        op = Op()
        cdeps = set()
        ddeps = {}
        for d in deps:
            dop = self.ops[d]
            if dop.dma is not None:
                ddeps[dop.dma] = self.dcnt[dop.dma]
            else:
                cdeps.add(d)
        op.eng, op.fn, op.deps, op.ddeps, op.dma = eng, fn, cdeps, ddeps, dma
        op.edeps = {d for d in e_all if self.ops[d].dma is None}
        op.eddeps = {self.ops[d].dma: self.dcnt[self.ops[d].dma] for d in e_all if self.ops[d].dma is not None}
        op.sig = None
        if dma is not None:
            self.dcnt[dma] = self.dcnt.get(dma, 0) + 16
            op.sig = ("dma:" + dma, self.dcnt[dma])
        self.ops.append(op)
        return idx

    def emit(self, nc, es):
        ops = self.ops
        needed = set()
        for op in ops:
            for d in op.deps:
                dop = ops[d]
                if dop.eng == "pe" and op.eng == "pe" and dop.dma is None and op.dma is None:
                    continue
                needed.add(d)
        cnt = {e: 0 for e in ENGS}
        dcnt = self.dcnt
        for i, op in enumerate(ops):
            if op.dma is not None:
                pass
            elif i in needed:
                cnt[op.eng] += 1
                op.sig = (op.eng, cnt[op.eng])
            else:
                op.sig = None
        waited = {e: {} for e in ENGS}
        for op in ops:
            need = {}
            for d in op.deps:
                dop = ops[d]
                if dop.eng == "pe" and op.eng == "pe" and dop.dma is None and op.dma is None:
                    continue
                s, v = dop.sig
                if need.get(s, 0) < v:
                    need[s] = v
            for k, v in op.ddeps.items():
                need["dma:" + k] = v
            w = waited[op.eng]
            op.waits = [(s, v) for s, v in need.items() if w.get(s, 0) < v]
            for s, v in op.waits:
                w[s] = v
            op.fuse = None
            if FUSE_WAITS and op.fn is not None and op.dma is None and op.waits:
                early_s = set()
                for d in op.edeps:
                    dop = ops[d]
                    if not (dop.eng == "pe" and op.eng == "pe"):
                        early_s.add(dop.sig[0])
                for k in op.eddeps:
                    early_s.add("dma:" + k)
                cands = [x for x in op.waits if x[0] not in early_s]
                if cands:
                    op.fuse = cands[-1]
                    op.waits = [x for x in op.waits if x is not op.fuse]
        semnames = list(ENGS) + sorted({"dma:" + k for k in dcnt})
        sems = {s: es.enter_context(nc.semaphore("s_" + s.replace(":", "_"))) for s in semnames}
        per = {e: [op for op in ops if op.eng == e] for e in ENGS}
        block = es.enter_context(nc.Block())

        def run(eng, lst):
            for op in lst:
                for s, v in op.waits:
                    eng.wait_ge(sems[s], v)
                if op.fn is None:
                    continue
                inst = op.fn(eng)
                if op.fuse is not None:
                    inst._wait_ge(sems[op.fuse[0]], op.fuse[1])
                if op.sig is not None:
                    inst.then_inc(sems[op.sig[0]], 16 if op.dma is not None else 1)

        @block.tensor
        def _(e):
            run(e, per["pe"])

        @block.scalar
        def _(e):
            run(e, per["act"])

        @block.vector
        def _(e):
            run(e, per["dve"])

        @block.gpsimd
        def _(e):
            run(e, per["pool"])

        @block.sync
        def _(e):
            run(e, per["sp"])

        self.stats = {e: len(per[e]) for e in ENGS}
        self.stats["sig"] = dict(cnt)


def make_tiles(nb, per, sb=0):
    out, b = [], sb
    first = (nb - sb) % per
    while b < nb:
        n = first if (b == sb and first) else per
        out.append((b * 128, n * 128))
        b += n
    return out


def build_program(NB, HALO, DEPTH, dbg=False):
    TC = NB * 128
    NOUT = (NB - HALO) * 128
    nc = bass.Bass("TRN2", target_bir_lowering=False)
    P = Prog()

    def din(name, shape, dt=F32):
        return nc.dram_tensor(name, list(shape), dt, kind="ExternalInput").ap()

    xT = din("xT", [D, TC])
    memT = din("memT", [D, MEM])
    posr = din("posr", [128, TC], I32)
    w1d = [din("ffn1_w1", [DEPTH, D, DFF]), din("ffn2_w1", [DEPTH, D, DFF])]
    w3d = [din("ffn1_w3", [DEPTH, D, DFF]), din("ffn2_w3", [DEPTH, D, DFF])]
    w2d = [din("ffn1_w2", [DEPTH, DFF, D]), din("ffn2_w2", [DEPTH, DFF, D])]
    wind = din("w_in", [DEPTH, D, DIN])
    woutd = din("w_out", [DEPTH, D, D])
    wmemd = din("w_mem_kv", [DEPTH, D, 512])
    colsd = din("cols", [128, DEPTH, NCOL])
    cbd = din("cbf", [128, NCB])
    cfd = din("cff", [128, 2])
    outT = nc.dram_tensor("outT", [D, NOUT], F32, kind="ExternalOutput").ap()
    ropeD = nc.dram_tensor("ropeD", [2, 128, TC], F32, kind="Internal").ap()
    memhD = nc.dram_tensor("memhD", [128, KC * MEM], BF16, kind="Internal").ap()
    dbgT = None
    if dbg:
        dbgT = nc.dram_tensor("dbgT", [4, D, TC], F32, kind="ExternalOutput").ap()

    es = ExitStack()
    with es:
        def sb(name, shape, dt):
            return es.enter_context(nc.sbuf_tensor(name, list(shape), dt))

        xs = sb("xs", [128, KC, TC], F32)
        BSZ = max(KC * TC, 20480)
        Bt = sb("Bt", [128, BSZ], BF16)
        nT = Bt[:, 0:KC * TC].rearrange("p (k t) -> p k t", k=KC)
        wout = Bt[:, 0:8192].rearrange("p (k n) -> p k n", k=KC)
        diag = Bt[:, 8192:8192 + 3968].rearrange("p (j m) -> p j m", j=31)
        wmem = Bt[:, 8192:8192 + 4096].rearrange("p (k n) -> p k n", k=KC)
        ntile = Bt[:, 12288:16384].rearrange("p (k t) -> p k t", k=KC)
        yT = Bt[:, 16384:20480].rearrange("p (k t) -> p k t", k=KC)
        memn = yT[:, :, 0:256]
        ring = sb("ring", [128, 13312], BF16)
        win_b = ring[:, 0:7168].rearrange("p (k n) -> p k n", k=KC)
        win_a = ring[:, 7168:13312].rearrange("p (k n) -> p k n", k=KC)

        def wcol(kc, c0, n=128):
            if c0 < 768:
                return win_a[:, kc, c0:c0 + n]
            return win_b[:, kc, c0 - 768:c0 - 768 + n]
        cols = sb("cols_sb", [128, DEPTH, NCOL], F32)
        cb = sb("cb", [128, NCB], BF16)
        cf = sb("cf", [128, 2], F32)
        S = sb("S", [128, 4, 512], F32)
        sqb = sb("sqb", [128, 2, 512], BF16)
        gT = sb("gT", [128, 2, 2, 512], BF16)
        cs = sb("cs", [128, 2, 512], F32)
        glu = sb("glu", [128, 3, 544], BF16)
        yf = sb("yf", [128, 3, 512], F32)
        ybf = sb("ybf", [128, 3, 512], BF16)
        qT = sb("qT", [128, 3, 512], BF16)
        qn = sb("qn", [128, 2, 512], BF16)
        kTz = sb("kTz", [128, 2, 640], BF16)
        Vz = sb("Vz", [128, 5, 2, 128], BF16)
        qmT = sb("qmT", [128, 2, 512], BF16)
        mkTz = sb("mkTz", [128, 4, 256], BF16)
        mvz = sb("mvz", [128, 2, 4, 128], BF16)
        EE = sb("EE", [128, 2, 4, 384], BF16)
        EEflat = EE[:, :, :, :].rearrange("p a b c -> p (a b c)")
        Emem = EEflat[:, 0:2048].rearrange("p (h x) -> p h x", h=2)
        esink = sb("esink", [128, 3], F32)
        ps = es.enter_context(nc.psum_tensor("ps", [128, 8, 512], F32))

        ident = cb[:, CB_ID:CB_ID + 128]
        ones = cb[:, CB_ONES:CB_ONES + 128]
        blk = cb[:, CB_BLK:CB_BLK + 128]
        rotm = cb[:, CB_ROT:CB_ROT + 128]
        oneg = [cb[:, CB_ONEG0:CB_ONEG0 + 128], cb[:, CB_ONEG1:CB_ONEG1 + 128]]
        maskC = cb[:, CB_MC:CB_MC + 384]
        maskP = cb[:, CB_MP:CB_MP + 384]
        maskPF = cb[:, CB_MPF:CB_MPF + 384]

        rr = {"h": 0, "f": 0}

        def ps_half(T=256):
            return ps_full(T)

        def ps_full(T=512):
            n = 6 if rr.get("ln_live") else 8
            f = rr["f"] % n
            rr["f"] = (f + 1) % n
            return ps[:, f, 0:T]

        srr = {"i": 0, "q": 0}

        def s_buf(T):
            i = srr["i"]
            srr["i"] = (i + 1) % 4
            return S[:, i, 0:T]

        def sq_buf(T):
            i = srr["q"]
            srr["q"] = (i + 1) % 2
            return sqb[:, i, 0:T]

        def mm(out, lhsT, rhs, start, stop):
            P.add("pe", lambda e: e.matmul(out, lhsT, rhs, start=start, stop=stop),
                  reads=[lhsT, rhs], writes=[out], early=[lhsT])

        def act(out, in_, func, bias=None, scale=None, extra=()):
            kw = {}
            if bias is not None:
                kw["bias"] = bias
            if scale is not None:
                kw["scale"] = scale
            rd = [in_] + list(extra)
            P.add("act", lambda e: e.activation(out, in_, func, **kw), reads=rd, writes=[out])

        def tt(out, a, b, op, eng="dve"):
            P.add(eng, lambda e: e.tensor_tensor(out, a, b, op), reads=[a, b], writes=[out])

        def ts(out, a, s1, op0, s2=None, op1=None, eng="dve"):
            rd = [a] + [s for s in (s1, s2) if not isinstance(s, (int, float, type(None)))]
            if op1 is None:
                P.add(eng, lambda e: e.tensor_scalar(out, a, s1, None, op0), reads=rd, writes=[out])
            else:
                P.add(eng, lambda e: e.tensor_scalar(out, a, s1, s2, op0, op1), reads=rd, writes=[out])

        def stt(out, a, s, b, op0, op1):
            rd = [a, b] + ([s] if not isinstance(s, (int, float)) else [])
            P.add("dve", lambda e: e.scalar_tensor_tensor(out, a, s, b, op0, op1), reads=rd, writes=[out])

        def cp(out, in_, eng="dve"):
            if eng == "act":
                P.add("act", lambda e: e.activation(out, in_, AF.Identity), reads=[in_], writes=[out])
            else:
                P.add(eng, lambda e: e.tensor_copy(out, in_), reads=[in_], writes=[out])

        swq = {"out": [], "tot": 0}
        SW_BUDGET = 10000

        def _rows(ap):
            n = 1
            for (_, c) in list(ap.ap)[:-1]:
                n *= c
            return n

        def dma(q, out, in_, key, reads=None, writes=None):
            rd = [in_] if reads is None else reads
            wr = [out] if writes is None else writes
            idx = P.add(q, lambda e: e.dma_start(out=out, in_=in_), reads=rd, writes=wr, dma=key)
            if q == "pool":
                nd = max(_rows(out), _rows(in_))
                op = P.ops[idx]
                while swq["out"] and swq["tot"] + nd > SW_BUDGET:
                    k_, v_, n_ = swq["out"].pop(0)
                    swq["tot"] -= n_
                    if op.ddeps.get(k_, 0) < v_:
                        op.ddeps[k_] = v_
                swq["out"].append((key, op.sig[1], nd))
                swq["tot"] += nd

        def memset(ap, val, eng="pool"):
            P.add(eng, lambda e: e.memset(ap, val), writes=[ap])

        def rstd_from(stat_ps, scale, T, to_sbuf=False):
            act(stat_ps, stat_ps, AF.Ln, bias=epsc, scale=scale, extra=[epsc])
            dst = s_buf(T) if to_sbuf else stat_ps
            act(dst, stat_ps, AF.Exp, scale=-0.5)
            return dst

        nsq = {"i": 0}

        def nsq_buf(T):
            pool_ = [sqb[:, 0, 0:T], sqb[:, 1, 0:T], ybf[:, 0, 0:T], ybf[:, 1, 0:T], ybf[:, 2, 0:T],
                     qn[:, 0, 0:T], qn[:, 1, 0:T], qmT[:, 0, 0:T]]
            i = nsq["i"]
            nsq["i"] = (i + 1) % 8
            return pool_[i]

        def rms_rstd(chunks, T, lhs=None, wide=True):
            st = ps_full(T)
            n = len(chunks)
            for i, x in enumerate(chunks):
                q = nsq_buf(T) if wide else sq_buf(T)
                act(q, x, AF.Square)
                mm(st, ones if lhs is None else lhs, q, i == 0, i == n - 1)
            return rstd_from(st, 1.0 / D, T)

        epsc = None
        dma("pool", cb[:, :], cbd[:, :], "constp", reads=["d:cbf"])
        dma("sp", cf[:, :], cfd[:, :], "const", reads=["d:cff"])
        dma("sp", cols[:, :, :], colsd[:, :, :], "const", reads=["d:cols"])
        epsc_t = sb("epsc", [128, 2], F32)
        memset(epsc_t[:, 0:1], EPS)
        memset(epsc_t[:, 1:2], math.pi / 2)
        epsc = epsc_t[:, 0:1]
        halfpi = epsc_t[:, 1:2]
        memset(kTz[:, :, :], 0.0)
        memset(Vz[:, :, :, :], 0.0)
        memset(mkTz[:, :, :], 0.0)
        memset(mvz[:, :, :, :], 0.0)
        memset(glu[:, :, :], 0.0)
        for kc in range(KC):
            dma("act", xs[:, kc, :], xT[kc * 128:(kc + 1) * 128, :], "xin", reads=["d:x"])

        wg = [(l, w, g) for l in range(DEPTH) for w in (0, 1) for g in range(NG)]
        wstate = {"next": 0}

        def slot_views(s):
            b0 = 0 if s == 0 else 7168
            w1g = ring[:, b0:b0 + 2048].rearrange("p (k f) -> p k f", k=KC)
            w3g = ring[:, b0 + 2048:b0 + 4096].rearrange("p (k f) -> p k f", k=KC)
            w2g = ring[:, b0 + 4096:b0 + 6144].rearrange("p (c n) -> p c n", c=2)
            return w1g, w3g, w2g

        def load_next_group(force=False):
            i = wstate["next"]
            if i < len(wg) and wg[i][1] == 1 and wg[i][2] < 2 and not force:
                return
            if i >= len(wg):
                return
            wstate["next"] = i + 1
            l, w, g = wg[i]
            s = i % 2
            w1g, w3g, w2g = slot_views(s)
            f0 = g * 256
            dma("pool", w1g, w1d[w][l, :, f0:f0 + 256].rearrange("(k p) f -> p k f", p=128), "ring%d" % s, reads=["d:w"])
            dma("pool", w3g, w3d[w][l, :, f0:f0 + 256].rearrange("(k p) f -> p k f", p=128), "ring%d" % s, reads=["d:w"])
            dma("pool", w2g, w2d[w][l, f0:f0 + 256, :].rearrange("(c p) n -> p c n", p=128), "ring%d" % s, reads=["d:w"])

        load_next_group()
        load_next_group()

        TWO_PI_HI = 6.28125
        TWO_PI_LO = 2.0 * math.pi - 6.28125
        for ri, (t0, T) in enumerate(make_tiles(NB, 4)):
            if ri % 2 == 0:
                B0, B1, B2, B3 = S[:, 0, 0:T], S[:, 1, 0:T], S[:, 2, 0:T], S[:, 3, 0:T]
            else:
                B0, B1, B2, B3 = yf[:, 0, 0:T], yf[:, 1, 0:T], yf[:, 2, 0:T], cs[:, 0, 0:T]
            pi_t = B0.bitcast(I32)
            dma("pool", pi_t, posr[:, t0:t0 + T], "ropein%d" % (ri % 2), reads=["d:pos"])
            ang = B1
            ts(ang, pi_t, cf[:, 0:1], ALU.mult)
            kf = B2
            ts(kf, ang, 1.0 / (2.0 * math.pi), ALU.mult)
            k2 = B3
            ts(k2, kf, 12582912.0, ALU.add)
            ts(kf, k2, -12582912.0, ALU.add)
            r = B3
            stt(r, kf, -TWO_PI_HI, ang, ALU.mult, ALU.add)
            stt(r, kf, -TWO_PI_LO, r, ALU.mult, ALU.add)
            ts(r, r, 3.1415925, ALU.min, -3.1415925, ALU.max)
            ng = B2
            ts(ng, r, -1.0, ALU.mult)
            ab = B2
            tt(ab, ng, r, ALU.max)
            sn = B1
            act(sn, r, AF.Sin)
            co = B0
            act(co, ab, AF.Sin, bias=halfpi, scale=-1.0, extra=[halfpi])
            dma("sp", ropeD[0, :, t0:t0 + T], co, "ropeout%d" % (ri % 2), writes=["d:ropeC%d" % ri])
            dma("sp", ropeD[1, :, t0:t0 + T], sn, "ropeout%d" % (ri % 2), writes=["d:ropeS%d" % ri])

        memh_sb = Bt[:, 0:KC * MEM].rearrange("p (k t) -> p k t", k=KC)
        for h in range(2):
            stg = S[:, 0:2, :].rearrange("p a (k t) -> p (a k) t", t=128)
            for kc in range(KC):
                dma("sp", stg[:, kc, :], memT[kc * 128:(kc + 1) * 128, h * 128:(h + 1) * 128], "memin", reads=["d:mem"])
            st = ps_full(128)
            for kc in range(KC):
                q = sq_buf(128)
                act(q, stg[:, kc, :], AF.Square)
                mm(st, ones, q, kc == 0, kc == KC - 1)
            lnv = S[:, 2, 0:128]
            act(lnv, st, AF.Ln, bias=epsc, scale=1.0 / D, extra=[epsc])
            act(st, lnv, AF.Exp, scale=-0.5)
            for kc in range(KC):
                tt(memh_sb[:, kc, h * 128:(h + 1) * 128], stg[:, kc, :], st, ALU.mult)

        dma("sp", memhD, memh_sb.rearrange("p k t -> p (k t)"), "memhout", writes=["d:memh"])

        def start_block(l):
            return max(0, HALO - (DEPTH - l))

        def norm_to(dst_fn, src_fn, T, gcol0, l, out_f32=False):
            chunks = [src_fn(kc) for kc in range(KC)]
            rs = rms_rstd(chunks, T)
            for kc in range(KC):
                stt(dst_fn(kc), chunks[kc], cols[:, l, gcol0 + kc:gcol0 + kc + 1], rs, ALU.mult, ALU.mult)

        def ffn_phase(l, w):
            ftiles = make_tiles(NB, 4, start_block(l) if w == 0 else start_block(l + 1))
            gbase = C_FFN1 if w == 0 else C_FFN2
            for (t0, T) in ftiles:
                norm_to(lambda kc: nT[:, kc, t0:t0 + T], lambda kc: xs[:, kc, t0:t0 + T], T, gbase, l)
            its = [(g, ti) for g in range(NG) for ti in range(len(ftiles))]
            gidx0 = (l * 2 + w) * NG
            par = {"o": 0}

            def emit_ab(k, c):
                g, ti = its[k]
                t0, T = ftiles[ti]
                s = (gidx0 + g) % 2
                w1g, w3g, _ = slot_views(s)
                a_ps = ps[:, c, 0:T]
                b_ps = ps[:, 2 + c, 0:T]
                for kc in range(KC):
                    mm(a_ps, w1g[:, kc, c * 128:(c + 1) * 128], nT[:, kc, t0:t0 + T], kc == 0, kc == KC - 1)
                for kc in range(KC):
                    mm(b_ps, w3g[:, kc, c * 128:(c + 1) * 128], nT[:, kc, t0:t0 + T], kc == 0, kc == KC - 1)
                sa = S[:, c, 0:T]
                act(sa, a_ps, AF.Silu)
                tt(gT[:, k % 2, c, 0:T], sa, b_ps, ALU.mult)

            def emit_out(k):
                g, ti = its[k]
                t0, T = ftiles[ti]
                s = (gidx0 + g) % 2
                _, _, w2g = slot_views(s)
                for dc in range(KC):
                    o = ps[:, 4 + par["o"], 0:T]
                    par["o"] = (par["o"] + 1) % 4
                    for c in range(2):
                        mm(o, w2g[:, c, dc * 128:(dc + 1) * 128], gT[:, k % 2, c, 0:T], c == 0, c == 1)
                    stt(xs[:, dc, t0:t0 + T], o, 0.5, xs[:, dc, t0:t0 + T], ALU.mult, ALU.add)
                if ti == len(ftiles) - 1:
                    load_next_group()

            n = len(its)
            emit_ab(0, 0)
            emit_ab(0, 1)
            for k in range(n):
                if k + 1 < n:
                    emit_ab(k + 1, 0)
                emit_out(k)
                if k + 1 < n:
                    emit_ab(k + 1, 1)

        def qk_norm(p_ps, T):
            q = sq_buf(T)
            act(q, p_ps, AF.Square)
            st = ps_full(T)
            mm(st, blk, q, True, True)
            return rstd_from(st, 1.0 / 64.0, T, to_sbuf=True)

        def mixer_phase(l):
            TM = 4
            dma("pool", wmem, wmemd[l].rearrange("(k p) n -> p k n", p=128), "wmem", reads=["d:w"])
            dma("pool", wout[:, 0:3, :], woutd[l, 0:384, :].rearrange("(k p) n -> p k n", p=128), "wout", reads=["d:w"])
            for j in range(3):
                for g in range(2):
                    r0 = 384 + 64 * (3 * g + j)
                    dma("pool", wout[64 * g:64 * g + 64, 3 + j, :], woutd[l, r0:r0 + 64, :], "wout", reads=["d:w"])
            dma("pool", wout[:, 6:8, :], woutd[l, 768:1024, :].rearrange("(k p) n -> p k n", p=128), "wout", reads=["d:w"])
            dma("sp", memn, memhD.rearrange("p (k t) -> p k t", k=KC), "memh", reads=["d:memh"])
            act(esink[:, :], cols[:, l, C_SINK:C_SINK + 3], AF.Exp)
            for kc in range(KC):
                ts(memn[:, kc, :], memn[:, kc, :], cols[:, l, C_MEMN + kc:C_MEMN + kc + 1], ALU.mult)
            def mem_kv_prep():
                for m_ in range(2):
                    pk = ps_full(256)
                    for kc in range(KC):
                        mm(pk, wmem[:, kc, m_ * 128:(m_ + 1) * 128], memn[:, kc, :], kc == 0, kc == KC - 1)
                    rs = qk_norm(pk, 256)
                    tmp = sq_buf(256)
                    stt(tmp, pk, cols[:, l, C_MKN:C_MKN + 1], rs, ALU.mult, ALU.mult)
                    for hh in range(2):
                        cp(mkTz[64 * hh:64 * hh + 64, 2 * m_ + hh, :], tmp[64 * hh:64 * hh + 64, :])
                for mc in range(2):
                    pv = ps_full(256)
                    for kc in range(KC):
                        mm(pv, memn[:, kc, mc * 128:(mc + 1) * 128], wmem[:, kc, 256:512], kc == 0, kc == KC - 1)
                    for h in range(4):
                        cp(mvz[:, mc, h, (h % 2) * 64:(h % 2) * 64 + 64], pv[:, h * 64:(h + 1) * 64], eng="act")


            mtiles = make_tiles(NB, TM, start_block(l))
            for ti, (t0, T) in enumerate(mtiles):
                nblk = T // 128
                rti = [i_ for i_, (a_, w_) in enumerate(make_tiles(NB, 4)) if a_ <= t0 < a_ + w_][0]
                dma("sp", cs[:, 0, 0:T], ropeD[0, :, t0:t0 + T], "rope", reads=["d:ropeC%d" % rti])
                dma("sp", cs[:, 1, 0:T], ropeD[1, :, t0:t0 + T], "rope", reads=["d:ropeS%d" % rti])
                cosT = cs[:, 0, 0:T]
                sinT = cs[:, 1, 0:T]
                norm_to(lambda kc: ntile[:, kc, 0:T], lambda kc: xs[:, kc, t0:t0 + T], T, C_MIX, l)

                def proj(col0):
                    o = ps_full(T)
                    for kc in range(KC):
                        mm(o, wcol(kc, col0), ntile[:, kc, 0:T], kc == 0, kc == KC - 1)
                    return o

                for c in range(3):
                    a_ps = proj(c * 128)
                    g_ps = proj(384 + c * 128)
                    sg = s_buf(T)
                    act(sg, g_ps, AF.Sigmoid)
                    tt(glu[:, c, 32:32 + T], sg, a_ps, ALU.mult)
                if HALO > 0:
                    hb0 = (HALO - 1) * 128
                    if t0 <= hb0 < t0 + T:
                        o_ = 32 + hb0 - t0
                        ts(glu[:, :, o_:o_ + 128], glu[:, :, o_:o_ + 128], cf[:, 1:2], ALU.mult)

                def build_diag(c):
                    for j in range(31):
                        ts(diag[:, j, :], ident, cols[:, l, C_CW + c * 31 + j:C_CW + c * 31 + j + 1], ALU.mult)

                if ti > 0:
                    build_diag(0)

                st1 = {}

                def s1(j):
                    col0 = 768 + j * 128 if j < 3 else 1152
                    p_ps = proj(col0)
                    st1[j] = (p_ps, qk_norm(p_ps, T))

                def s2(j):
                    p_ps, rs = st1.pop(j)
                    gcol = C_QN if j < 3 else C_KN
                    qb = qn[:, j % 2, 0:T]
                    stt(qb, p_ps, cols[:, l, gcol:gcol + 1], rs, ALU.mult, ALU.mult)
                    rp = ps_full(T)
                    mm(rp, rotm, qb, True, True)
                    t1 = s_buf(T)
                    tt(t1, qb, cosT, ALU.mult)
                    t2 = s_buf(T)
                    tt(t2, rp, sinT, ALU.mult)
                    if j < 3:
                        tt(qT[:, j, 0:T], t1, t2, ALU.add)
                    else:
                        tt(kTz[0:64, 0, 128:128 + T], t1[0:64, :], t2[0:64, :], ALU.add)
                        tt(kTz[64:128, 1, 128:128 + T], t1[64:128, :], t2[64:128, :], ALU.add)

                s1(0)
                for j in range(1, 4):
                    s1(j)
                    s2(j - 1)
                pvb = ps_full(T)
                for bi in range(nblk):
                    for kc in range(KC):
                        mm(pvb[:, bi * 128:(bi + 1) * 128], ntile[:, kc, bi * 128:(bi + 1) * 128], wcol(kc, 1280),
                           kc == 0, kc == KC - 1)
                s2(3)
                pv3 = pvb.rearrange("p (b f) -> p b f", f=128)
                cp(Vz[:, 1:1 + nblk, 0, 0:64], pv3[:, :, 0:64], eng="act")
                cp(Vz[:, 1:1 + nblk, 1, 64:128], pv3[:, :, 64:128], eng="act")
                qm_ps = [proj(1408 + m_ * 128) for m_ in range(2)]
                for m_ in range(2):
                    rs = qk_norm(qm_ps[m_], T)
                    stt(qmT[:, m_, 0:T], qm_ps[m_], cols[:, l, C_MQN:C_MQN + 1], rs, ALU.mult, ALU.mult)
                if ti == 0:
                    mem_kv_prep()
                    build_diag(0)

                sum_ps = ps[:, 6, 0:T]
                sq_ps = ps[:, 7, 0:T]

                def conv_chunk(c):
                    y_ps = ps_full(T)
                    for j in range(31):
                        mm(y_ps, diag[:, j, :], glu[:, c, 2 + j:2 + j + T], j == 0, j == 30)
                    bcol = cols[:, l, C_CB + c:C_CB + c + 1]
                    act(ybf[:, c, 0:T], y_ps, AF.Identity, bias=bcol, extra=[bcol])
                    q = sq_buf(T)
                    act(q, y_ps, AF.Square, bias=bcol, extra=[bcol])
                    ts(yf[:, c, 0:T], y_ps, bcol, ALU.add)
                    if c < 2:
                        build_diag(c + 1)

                    def stats():
                        mm(sum_ps, ones, ybf[:, c, 0:T], c == 0, c == 2)
                        mm(sq_ps, ones, q, c == 0, c == 2)
                    return stats

                def swa_S(bi, half):
                    Esw = EE[:, half]
                    blk_id = (t0 // 128) + bi
                    mprev = maskPF if blk_id == HALO else maskP
                    for g in range(2):
                        for pc in range(2):
                            s_ps = ps_full(384)
                            kk = kTz[:, g, (bi + pc) * 128:(bi + pc + 1) * 128]
                            mm(s_ps.rearrange("p (j t) -> p j t", j=3), kk, qT[:, :, bi * 128:(bi + 1) * 128], True, False)
                            mm(s_ps, ident, maskC if pc == 1 else mprev, False, True)
                            act(Esw[:, g * 2 + pc, :], s_ps, AF.Exp, scale=0.125)

                def swa_PV(bi, half):
                    Esw = EE[:, half]
                    o_ps = ps_full(384)
                    d_ps = ps_full(384)
                    n_ = 0
                    for g in range(2):
                        for pc in range(2):
                            mm(o_ps, Vz[:, bi + pc, g, :], Esw[:, g * 2 + pc, :], n_ == 0, n_ == 3)
                            n_ += 1
                    n_ = 0
                    for g in range(2):
                        for pc in range(2):
                            mm(d_ps, oneg[g], Esw[:, g * 2 + pc, :], n_ == 0, n_ == 3)
                            n_ += 1
                    lnd = s_buf(384)
                    for j in range(3):
                        act(lnd[:, j * 128:(j + 1) * 128], d_ps[:, j * 128:(j + 1) * 128], AF.Ln,
                            bias=esink[:, j:j + 1], extra=[esink[:, j:j + 1]])
                    rec = s_buf(384)
                    act(rec, lnd, AF.Exp, scale=-1.0)
                    tt(yT[:, 3:6, bi * 128:(bi + 1) * 128], o_ps.rearrange("p (j t) -> p j t", j=3),
                       rec.rearrange("p (j t) -> p j t", j=3), ALU.mult)

                def swa_group(bis):
                    for i_, bi in enumerate(bis):
                        swa_S(bi, i_ % 2)
                        if i_ > 0:
                            swa_PV(bis[i_ - 1], (i_ - 1) % 2)
                    if bis:
                        swa_PV(bis[-1], (len(bis) - 1) % 2)

                def mem_attn():
                    for pr in range(2):
                        o_ps = ps_full(T)
                        d_ps = ps_full(T)
                        for hh in range(2):
                            h = 2 * pr + hh
                            E = Emem[:, hh].rearrange("p (m t) -> p m t", m=2)
                            for mc in range(2):
                                s_ps = ps_full(T)
                                mm(s_ps, mkTz[:, h, mc * 128:(mc + 1) * 128], qmT[:, pr, 0:T], True, True)
                                act(E[:, mc, 0:T], s_ps, AF.Exp, scale=0.125)
                        for hh in range(2):
                            h = 2 * pr + hh
                            E = Emem[:, hh].rearrange("p (m t) -> p m t", m=2)
                            for mc in range(2):
                                mm(o_ps, mvz[:, mc, h, :], E[:, mc, 0:T], (hh == 0 and mc == 0), (hh == 1 and mc == 1))
                        for hh in range(2):
                            E = Emem[:, hh].rearrange("p (m t) -> p m t", m=2)
                            for mc in range(2):
                                mm(d_ps, oneg[hh], E[:, mc, 0:T], (hh == 0 and mc == 0), (hh == 1 and mc == 1))
                        act(d_ps, d_ps, AF.Ln)
                        rec = s_buf(T)
                        act(rec, d_ps, AF.Exp, scale=-1.0)
                        tt(yT[:, 6 + pr, 0:T], o_ps, rec, ALU.mult)


                def ln_chain():
                    mean = s_buf(T)
                    ts(mean, sum_ps, 1.0 / 384.0, ALU.mult)
                    m2 = s_buf(T)
                    tt(m2, mean, mean, ALU.mult)
                    var = s_buf(T)
                    stt(var, sq_ps, 1.0 / 384.0, m2, ALU.mult, ALU.subtract)
                    ts(var, var, 0.0, ALU.max)
                    act(var, var, AF.Ln, bias=epsc, extra=[epsc])
                    act(sq_ps, var, AF.Exp, scale=-0.5)
                    for c in range(3):
                        tt(yf[:, c, 0:T], yf[:, c, 0:T], mean, ALU.subtract)
                        tt(yf[:, c, 0:T], yf[:, c, 0:T], sq_ps, ALU.mult)
                        gcol_ = cols[:, l, C_LG + c:C_LG + c + 1]
                        bcol_ = cols[:, l, C_LB + c:C_LB + c + 1]
                        act(yT[:, c, 0:T], yf[:, c, 0:T], AF.Silu, bias=bcol_, scale=gcol_, extra=[gcol_, bcol_])


                per = (nblk + 1) // 2
                rr["ln_live"] = True
                st0 = conv_chunk(0)
                swa_group(list(range(0, per)))
                st0()
                st1 = conv_chunk(1)
                swa_group(list(range(per, nblk)))
                st1()
                conv_chunk(2)()
                cp(kTz[:, :, 0:128], kTz[:, :, T:T + 128], eng="pool")
                cp(Vz[:, 0, :, :], Vz[:, nblk, :, :], eng="pool")
                cp(glu[:, :, 0:32], glu[:, :, T:T + 32], eng="pool")
                mem_attn()
                ln_chain()
                rr["ln_live"] = False

                for dc in range(KC):
                    o = ps_full(T)
                    for yc in range(KC):
                        mm(o, wout[:, yc, dc * 128:(dc + 1) * 128], yT[:, yc, 0:T], yc == 0, yc == KC - 1)
                    tt(xs[:, dc, t0:t0 + T], o, xs[:, dc, t0:t0 + T], ALU.add)

        outkeys = []

        def final_norm(l, last):
            for (t0, T) in make_tiles(NB, 4, start_block(l + 1)):
                norm_to(lambda kc: xs[:, kc, t0:t0 + T], lambda kc: xs[:, kc, t0:t0 + T], T, C_FIN, l)
                if last:
                    o0 = t0 - HALO * 128
                    for kc in range(KC):
                        k_ = "d:out%d_%d" % (kc, t0)
                        outkeys.append(k_)
                        dma("sp", outT[kc * 128:(kc + 1) * 128, o0:o0 + T], xs[:, kc, t0:t0 + T], "out", writes=[k_])

        def dbg_dump(i):
            if dbg:
                for kc in range(KC):
                    dma("sp", dbgT[i, kc * 128:(kc + 1) * 128, :], xs[:, kc, :], "dbg%d_%d" % (i, kc), writes=["d:dbg%d_%d" % (i, kc)])

        import os
        KST = int(os.environ.get("KSTAGE", "9"))
        nout = 0
        for l in range(DEPTH):
            if KST < 2:
                break
            ffn_phase(l, 0)
            if KST < 3:
                break
            if l == 0:
                dbg_dump(0)
            wv = wind[l].rearrange("(k p) n -> p k n", p=128)
            dma("pool", win_a, wv[:, :, 0:768], "ring1", reads=["d:w"])
            dma("pool", win_b, wv[:, :, 768:1664], "ring0", reads=["d:w"])
            qtmp = EEflat[:, 0:512].rearrange("p (k d) -> p k d", k=KC)

            def qblk(i):
                return win_b[:, :, i * 64:(i + 1) * 64]

            cp(qtmp, qblk(1), eng="pool")
            cp(qblk(1), qblk(3), eng="pool")
            cp(qblk(3), qblk(4), eng="pool")
            cp(qblk(4), qblk(2), eng="pool")
            cp(qblk(2), qtmp, eng="pool")
            mixer_phase(l)
            load_next_group(force=True)
            load_next_group(force=True)
            if l == 0:
                dbg_dump(1)
            if KST < 4:
                break
            ffn_phase(l, 1)
            if l == 0:
                dbg_dump(2)
            final_norm(l, l == DEPTH - 1)
            if l == 0:
                dbg_dump(3)
        allk = list(outkeys)
        if dbg:
            allk += ["d:dbg%d_%d" % (i, kc) for i in range(4) for kc in range(KC)]
        P.add("sp", None, reads=allk)
        P.emit(nc, es)
    return nc, P


def host_consts(first):
    cbf = np.zeros((128, NCB), np.float32)
    cbf[:, CB_ID:CB_ID + 128] = np.eye(128, dtype=np.float32)
    cbf[:, CB_ONES:CB_ONES + 128] = 1.0
    for b in range(2):
        cbf[64 * b:64 * b + 64, CB_BLK + 64 * b:CB_BLK + 64 * b + 64] = 1.0
    for b in range(2):
        for m in range(64):
            if m < 32:
                cbf[64 * b + m + 32, CB_ROT + 64 * b + m] = -1.0
            else:
                cbf[64 * b + m - 32, CB_ROT + 64 * b + m] = 1.0
    cbf[:, CB_ONEG0:CB_ONEG0 + 64] = 1.0
    cbf[:, CB_ONEG1 + 64:CB_ONEG1 + 128] = 1.0
    j = np.arange(128)[:, None]
    i = np.arange(128)[None, :]
    mc = np.where(j <= i, 0.0, NEG).astype(np.float32)
    mp = np.where(j > i, 0.0, NEG).astype(np.float32)
    cbf[:, CB_MC:CB_MC + 384] = np.tile(mc, (1, 3))
    cbf[:, CB_MP:CB_MP + 384] = np.tile(mp, (1, 3))
    cbf[:, CB_MPF:CB_MPF + 384] = NEG if first else np.tile(mp, (1, 3))
    cff = np.zeros((128, 2), np.float32)
    inv = (10000.0 ** (-(np.arange(0, 64, 2, dtype=np.float64)) / 64.0)).astype(np.float32)
    cff[:, 0] = np.tile(inv, 4)
    cff[:, 1] = 0.0 if first else 1.0
    return cbf, cff


def pack_cols(p, layers):
    L = len(layers)
    cols = np.zeros((128, L, NCOL), np.float32)
    for li, l in enumerate(layers):
        def dm(v):
            return np.asarray(v, np.float32).reshape(8, 128).T
        cols[:, li, C_FFN1:C_FFN1 + 8] = dm(p["ffn1_norm"][l])
        cols[:, li, C_MIX:C_MIX + 8] = dm(p["mix_norm"][l])
        cols[:, li, C_FFN2:C_FFN2 + 8] = dm(p["ffn2_norm"][l])
        cols[:, li, C_FIN:C_FIN + 8] = dm(p["final_norm"][l])
        cols[:, li, C_MEMN:C_MEMN + 8] = dm(p["mem_norm"][l])
        cw = np.asarray(p["conv_w"][l], np.float32)
        for c in range(3):
            cols[:, li, C_CW + c * 31:C_CW + (c + 1) * 31] = cw[:, c * 128:(c + 1) * 128].T
        cols[:, li, C_CB:C_CB + 3] = np.asarray(p["conv_b"][l], np.float32).reshape(3, 128).T
        cols[:, li, C_LG:C_LG + 3] = np.asarray(p["conv_ln_g"][l], np.float32).reshape(3, 128).T
        cols[:, li, C_LB:C_LB + 3] = np.asarray(p["conv_ln_b"][l], np.float32).reshape(3, 128).T
        cols[:, li, C_QN] = np.tile(np.asarray(p["swa_q_norm"][l], np.float32), 2)
        cols[:, li, C_KN] = np.tile(np.asarray(p["swa_k_norm"][l], np.float32), 2)
        cols[:, li, C_MQN] = np.tile(np.asarray(p["mem_q_norm"][l], np.float32), 2)
        cols[:, li, C_MKN] = np.tile(np.asarray(p["mem_k_norm"][l], np.float32), 2)
        sk = np.asarray(p["swa_sinks"][l], np.float32)
        for j in range(3):
            cols[0:64, li, C_SINK + j] = sk[j]
            cols[64:128, li, C_SINK + j] = sk[3 + j]
    return cols


_PROG_CACHE = {}


def get_program(NB, HALO, DEPTH, dbg=False):
    key = (NB, HALO, DEPTH, dbg)
    if key not in _PROG_CACHE:
        _PROG_CACHE[key] = build_program(NB, HALO, DEPTH, dbg)
    return _PROG_CACHE[key][0]


WNAMES = ["ffn1_w1", "ffn1_w3", "ffn1_w2", "ffn2_w1", "ffn2_w3", "ffn2_w2", "w_in", "w_out", "w_mem_kv"]


def run_layers(x, mem, positions, p, layers, HALO, n_real_blocks=16, ncore_per_seq=4, dbg=False):
    B, S_, _ = x.shape
    NB = n_real_blocks + HALO
    TC = NB * 128
    nreal = n_real_blocks * 128
    ncores = B * ncore_per_seq
    nc = get_program(NB, HALO, len(layers), dbg)
    cols = pack_cols(p, layers)
    wsl = {k: np.ascontiguousarray(np.asarray(p[k], np.float32)[layers]) for k in WNAMES}
    in_maps = []
    for c in range(ncores):
        b, qd = divmod(c, ncore_per_seq)
        start = qd * nreal - HALO * 128
        xc = np.zeros((TC, D), np.float32)
        pc = np.zeros((TC,), np.int32)
        lo = max(start, 0)
        xc[lo - start:] = x[b, lo:start + TC]
        pc[lo - start:] = positions[b, lo:start + TC]
        cbf, cff = host_consts(first=(qd == 0))
        m = {
            "xT": np.ascontiguousarray(xc.T),
            "memT": np.ascontiguousarray(np.asarray(mem[b], np.float32).T),
            "posr": np.ascontiguousarray(np.broadcast_to(pc[None, :], (128, TC))),
            "cols": cols, "cbf": cbf, "cff": cff,
        }
        m.update(wsl)
        in_maps.append(m)
    res = run_bass_kernel_spmd(nc, in_maps, core_ids=list(range(ncores)))
    out = np.zeros((B, ncore_per_seq * nreal, D), np.float32)
    for c in range(ncores):
        b, qd = divmod(c, ncore_per_seq)
        out[b, qd * nreal:(qd + 1) * nreal] = res.results[c]["outT"].T
    return out, res


FUSED = True


def kernel(**inputs):
    p = {k: np.asarray(v) for k, v in inputs.items()}
    x = np.asarray(p["x"], np.float32)
    mem = np.asarray(p["mem"], np.float32)
    pos = np.asarray(p["positions"], np.int32)
    L = p["ffn1_w1"].shape[0]
    if FUSED:
        out, _ = run_layers(x, mem, pos, p, list(range(L)), HALO=L)
        return out
    for l in range(L):
        x, _ = run_layers(x, mem, pos, p, [l], HALO=1)
    return x
```
